# Optimizing a Trainium2 kernel written in Bass

```python
import jax, jax.numpy as jnp
from jax import lax
import numpy as np

D_MODEL = 1024
BATCH = 8
SEQ = 4096
DEPTH = 1

GRID_W = 64
CTX_LEN = 256
EPS = 1e-6

LRU_WIDTH = 1280
LRU_BLOCKS = 8
LRU_BLOCK_DIM = LRU_WIDTH // LRU_BLOCKS
LRU_C = 8.0
CONV_WIDTH = 4
CONV_LEFT = 2

HEAD_DIM = 128
N_HEADS = D_MODEL // HEAD_DIM
N_KV_HEADS = 2
GROUP = N_HEADS // N_KV_HEADS
ATTN_WIDTH = N_HEADS * HEAD_DIM
KV_WIDTH = N_KV_HEADS * HEAD_DIM
ROPE_AXIS_DIM = HEAD_DIM // 2
ROPE_THETA = 10000.0
Q_BLOCK = 128

FFN_HIDDEN = ((8 * D_MODEL + 3 * 256 - 1) // (3 * 256)) * 256

IN_WIDTHS = (LRU_WIDTH, LRU_WIDTH, ATTN_WIDTH, KV_WIDTH, KV_WIDTH, D_MODEL, D_MODEL)
IN_COLS = 2 * LRU_WIDTH + ATTN_WIDTH + 2 * KV_WIDTH + 2 * D_MODEL

kernel_name = 'hybrid_rglru_gqa_dit_block'


def rmsnorm(x, g):
    xf = x.astype(jnp.float32)
    y = xf * lax.rsqrt(jnp.mean(xf * xf, axis=-1, keepdims=True) + EPS)
    return (y * g.astype(jnp.float32)).astype(x.dtype)


def modulate(h, shift, scale):
    return h * (1.0 + scale) + shift


def split_in(p):
    idx = []
    acc = 0
    for w in IN_WIDTHS[:-1]:
        acc += w
        idx.append(acc)
    return jnp.split(p, idx, axis=-1)


def axial_rope_tables(n_tokens):
    rows = n_tokens // GRID_W
    row = jnp.repeat(jnp.arange(rows, dtype=jnp.float32), GRID_W)
    col = jnp.tile(jnp.arange(GRID_W, dtype=jnp.float32), rows)
    inv_freq = ROPE_THETA ** (-jnp.arange(0, ROPE_AXIS_DIM, 2, dtype=jnp.float32) / ROPE_AXIS_DIM)
    ang_r = row[:, None] * inv_freq[None, :]
    ang_c = col[:, None] * inv_freq[None, :]
    expand = lambda t: t[None, :, None, :]
    return (expand(jnp.cos(ang_r)), expand(jnp.sin(ang_r)), expand(jnp.cos(ang_c)), expand(jnp.sin(ang_c)))


def rotate_axis(x, cos, sin):
    half = x.shape[-1] // 2
    x1, x2 = x[..., :half], x[..., half:]
    return jnp.concatenate([x1 * cos - x2 * sin, x2 * cos + x1 * sin], axis=-1)


def apply_axial_rope(x, tables):
    cr, sr, cc, sc = tables
    xf = x.astype(jnp.float32)
    out = jnp.concatenate([rotate_axis(xf[..., :ROPE_AXIS_DIM], cr, sr),
                           rotate_axis(xf[..., ROPE_AXIS_DIM:], cc, sc)], axis=-1)
    return out.astype(x.dtype)


def heads_prep(t, n_heads, gain, tables):
    b, l, _ = t.shape
    t = rmsnorm(t.reshape(b, l, n_heads, HEAD_DIM), gain)
    if tables is not None:
        t = apply_axial_rope(t, tables)
    return t


def block_attention(q, k, v):
    b, l, _, d = q.shape
    nb = l // Q_BLOCK
    qb = q.reshape(b, nb, Q_BLOCK, N_KV_HEADS, GROUP, d).transpose(1, 0, 3, 4, 2, 5)
    kt = k.transpose(0, 2, 1, 3).astype(jnp.float32)
    vt = v.transpose(0, 2, 1, 3).astype(jnp.float32)
    scale = 1.0 / np.sqrt(HEAD_DIM).astype(np.float32)

    def one_block(qi):
        s = jnp.einsum('bkgqd,bksd->bkgqs', qi.astype(jnp.float32), kt) * scale
        p = jax.nn.softmax(s, axis=-1)
        return jnp.einsum('bkgqs,bksd->bkgqd', p, vt).astype(v.dtype)

    o = lax.map(one_block, qb)
    return o.transpose(1, 0, 4, 2, 3, 5).reshape(b, l, N_HEADS * d)


def dw_conv(u, w, bias):
    l = u.shape[1]
    up = jnp.pad(u, ((0, 0), (CONV_LEFT, CONV_WIDTH - 1 - CONV_LEFT), (0, 0)))
    out = bias
    for j in range(CONV_WIDTH):
        out = out + up[:, j:j + l] * w[j]
    return out


def rglru_coeffs(xc, wa, ba, wx, bx, lam):
    b, l, _ = xc.shape
    xf = xc.astype(jnp.float32)
    xb = xf.reshape(b, l, LRU_BLOCKS, LRU_BLOCK_DIM)
    r = jax.nn.sigmoid(jnp.einsum('blnd,nde->blne', xb, wa.astype(jnp.float32)).reshape(b, l, LRU_WIDTH) + ba)
    i = jax.nn.sigmoid(jnp.einsum('blnd,nde->blne', xb, wx.astype(jnp.float32)).reshape(b, l, LRU_WIDTH) + bx)
    log_a = -LRU_C * r * jax.nn.softplus(-lam.astype(jnp.float32))
    a = jnp.exp(log_a)
    coef = jnp.sqrt(-jnp.expm1(2.0 * log_a))
    return a, coef * (i * xf)


def scan_combine(left, right):
    a_l, b_l = left
    a_r, b_r = right
    return a_r * a_l, a_r * b_l + b_r


def linear_scan(a, b, h0, reverse):
    edge = -1 if reverse else 0
    b = b.at[:, edge].add(a[:, edge] * h0)
    _, h = lax.associative_scan(scan_combine, (a, b), reverse=reverse, axis=1)
    return h


def bidirectional_rglru(xc_lat, xc_ctx, wa, ba, wx, bx, lam):
    bsz = xc_lat.shape[0]
    h_lat = jnp.zeros(xc_lat.shape, jnp.float32)
    h_ctx = jnp.zeros(xc_ctx.shape, jnp.float32)
    for d, reverse in enumerate((False, True)):
        a_c, b_c = rglru_coeffs(xc_ctx, wa[d], ba[d], wx[d], bx[d], lam[d])
        hc = linear_scan(a_c, b_c, jnp.zeros((bsz, LRU_WIDTH), jnp.float32), reverse)
        h_final = hc[:, 0] if reverse else hc[:, -1]
        a_l, b_l = rglru_coeffs(xc_lat, wa[d], ba[d], wx[d], bx[d], lam[d])
        hl = linear_scan(a_l, b_l, h_final, reverse)
        h_lat = h_lat + hl
        h_ctx = h_ctx + hc
    return h_lat.astype(xc_lat.dtype), h_ctx.astype(xc_ctx.dtype)


def hybrid_mixer(h, hc, tables, with_ctx_out, w_in, conv_w, conv_b, lru_wa, lru_ba, lru_wx, lru_bx,
                 lru_lambda, q_norm, k_norm, w_out_lru, w_out_attn, w_out):
    u, g, q, k, v, ga, gb = split_in(h @ w_in)
    uc, gc, qc, kc, vc, gac, gbc = split_in(hc @ w_in)

    xl = dw_conv(u, conv_w, conv_b)
    xcx = dw_conv(uc, conv_w, conv_b)
    lru_lat, lru_ctx = bidirectional_rglru(xl, xcx, lru_wa, lru_ba, lru_wx, lru_bx, lru_lambda)
    ya = (lru_lat * jax.nn.gelu(g)) @ w_out_lru

    q_l = heads_prep(q, N_HEADS, q_norm, tables)
    k_l = heads_prep(k, N_KV_HEADS, k_norm, tables)
    v_l = v.reshape(v.shape[0], v.shape[1], N_KV_HEADS, HEAD_DIM)
    k_c = heads_prep(kc, N_KV_HEADS, k_norm, None)
    v_c = vc.reshape(vc.shape[0], vc.shape[1], N_KV_HEADS, HEAD_DIM)
    attn_lat = block_attention(q_l, jnp.concatenate([k_c, k_l], axis=1), jnp.concatenate([v_c, v_l], axis=1))
    yb = attn_lat @ w_out_attn

    y = (jax.nn.sigmoid(ga) * ya + jax.nn.sigmoid(gb) * yb) @ w_out

    y_ctx = None
    if with_ctx_out:
        ya_c = (lru_ctx * jax.nn.gelu(gc)) @ w_out_lru
        q_c = heads_prep(qc, N_HEADS, q_norm, None)
        yb_c = block_attention(q_c, k_c, v_c) @ w_out_attn
        y_ctx = (jax.nn.sigmoid(gac) * ya_c + jax.nn.sigmoid(gbc) * yb_c) @ w_out
    return y, y_ctx


def swiglu(h, w_ffn_in, w_ffn_out):
    gate, up = jnp.split(h @ w_ffn_in, 2, axis=-1)
    return (jax.nn.silu(gate) * up) @ w_ffn_out


def setup_inputs(seed: int = 0) -> dict:
    key = jax.random.key(seed)
    ks = jax.random.split(key, 24)

    def nrm(k, shape, scale):
        return jax.random.normal(k, shape, jnp.float32) * scale

    u = jax.random.uniform(ks[10], (DEPTH, 2, LRU_WIDTH), jnp.float32, minval=0.9, maxval=0.999)
    p = u ** (1.0 / LRU_C)
    lam = jnp.log(p) - jnp.log1p(-p)
    return {
        'x': nrm(ks[0], (BATCH, SEQ, D_MODEL), 1.0),
        'c': nrm(ks[1], (BATCH, D_MODEL), 1.0),
        'ctx': nrm(ks[2], (BATCH, CTX_LEN, D_MODEL), 1.0),
        'c_ctx': nrm(ks[3], (D_MODEL,), 1.0),
        'w_mod': nrm(ks[4], (DEPTH, D_MODEL, 6 * D_MODEL), 0.5 * D_MODEL ** -0.5),
        'b_mod': nrm(ks[5], (DEPTH, 6 * D_MODEL), 0.01),
        'norm_mix': 1.0 + nrm(ks[6], (DEPTH, D_MODEL), 0.02),
        'w_in': nrm(ks[7], (DEPTH, D_MODEL, IN_COLS), D_MODEL ** -0.5),
        'conv_w': nrm(ks[8], (DEPTH, CONV_WIDTH, LRU_WIDTH), CONV_WIDTH ** -0.5),
        'conv_b': nrm(ks[9], (DEPTH, LRU_WIDTH), 0.01),
        'lru_wa': nrm(ks[11], (DEPTH, 2, LRU_BLOCKS, LRU_BLOCK_DIM, LRU_BLOCK_DIM), LRU_BLOCK_DIM ** -0.5),
        'lru_ba': nrm(ks[12], (DEPTH, 2, LRU_WIDTH), 0.01),
        'lru_wx': nrm(ks[13], (DEPTH, 2, LRU_BLOCKS, LRU_BLOCK_DIM, LRU_BLOCK_DIM), LRU_BLOCK_DIM ** -0.5),
        'lru_bx': nrm(ks[14], (DEPTH, 2, LRU_WIDTH), 0.01),
        'lru_lambda': lam,
        'q_norm': 1.0 + nrm(ks[15], (DEPTH, HEAD_DIM), 0.02),
        'k_norm': 1.0 + nrm(ks[16], (DEPTH, HEAD_DIM), 0.02),
        'w_out_lru': nrm(ks[17], (DEPTH, LRU_WIDTH, D_MODEL), LRU_WIDTH ** -0.5),
        'w_out_attn': nrm(ks[18], (DEPTH, ATTN_WIDTH, D_MODEL), ATTN_WIDTH ** -0.5),
        'w_out': nrm(ks[19], (DEPTH, D_MODEL, D_MODEL), D_MODEL ** -0.5),
        'norm_ffn': 1.0 + nrm(ks[20], (DEPTH, D_MODEL), 0.02),
        'w_ffn_in': nrm(ks[21], (DEPTH, D_MODEL, 2 * FFN_HIDDEN), D_MODEL ** -0.5),
        'w_ffn_out': nrm(ks[22], (DEPTH, FFN_HIDDEN, D_MODEL), FFN_HIDDEN ** -0.5),
    }


def reference(x, c, ctx, c_ctx, w_mod, b_mod, norm_mix, w_in, conv_w, conv_b, lru_wa, lru_ba, lru_wx,
              lru_bx, lru_lambda, q_norm, k_norm, w_out_lru, w_out_attn, w_out, norm_ffn, w_ffn_in, w_ffn_out):
    tables = axial_rope_tables(x.shape[1])
    silu_c = jax.nn.silu(c)
    silu_cc = jax.nn.silu(c_ctx)
    for i in range(DEPTH):
        last = i == DEPTH - 1
        mod = (silu_c @ w_mod[i] + b_mod[i])[:, None, :]
        mod_c = silu_cc @ w_mod[i] + b_mod[i]
        sh_m, sc_m, g_m, sh_f, sc_f, g_f = jnp.split(mod, 6, axis=-1)
        shc_m, scc_m, gc_m, shc_f, scc_f, gc_f = jnp.split(mod_c, 6, axis=-1)

        h = modulate(rmsnorm(x, norm_mix[i]), sh_m, sc_m)
        hc = modulate(rmsnorm(ctx, norm_mix[i]), shc_m, scc_m)
        y, y_ctx = hybrid_mixer(h, hc, tables, not last, w_in[i], conv_w[i], conv_b[i], lru_wa[i], lru_ba[i],
                                lru_wx[i], lru_bx[i], lru_lambda[i], q_norm[i], k_norm[i], w_out_lru[i],
                                w_out_attn[i], w_out[i])
        x = x + g_m * y
        x = x + g_f * swiglu(modulate(rmsnorm(x, norm_ffn[i]), sh_f, sc_f), w_ffn_in[i], w_ffn_out[i])
        if not last:
            ctx = ctx + gc_m * y_ctx
            ctx = ctx + gc_f * swiglu(modulate(rmsnorm(ctx, norm_ffn[i]), shc_f, scc_f), w_ffn_in[i], w_ffn_out[i])
    return x
```

```python
from contextlib import ExitStack

import numpy as np
import ml_dtypes

import concourse.bass as bass
import concourse.mybir as mybir
from concourse.ap import AP
from concourse.bass_utils import run_bass_kernel_spmd

F32 = mybir.dt.float32
BF16 = mybir.dt.bfloat16
AF = mybir.ActivationFunctionType
ALU = mybir.AluOpType

CE = ("pe", "act", "dve", "pool")
ALLE = ("pe", "act", "dve", "pool", "sp")

D = 1024
L = 4096
NCTX = 256
T = NCTX + L
LW = 1280
HID = 2816
EPS = 1e-6
NP = 192


class Reg:
    __slots__ = ("w", "r", "rd")

    def __init__(self):
        self.w = None
        self.r = {}
        self.rd = []


class Op:
    __slots__ = ("eng", "fn", "dma", "signal", "sig", "waits", "clock")


class Sched:
    def __init__(self, n_dma_sems=32):
        self.streams = {e: [] for e in ALLE}
        self.seq = {e: 0 for e in CE}
        self.known = {e: {} for e in ALLE}
        self.nd = n_dma_sems
        self.dma_use = [0] * n_dma_sems
        self.dma_last = [None] * n_dma_sems
        self.dma_rr = 0
        self.last_op = {e: None for e in ALLE}

    def _wait(self, eng, d, waits):
        key, val = d.sig
        kn = self.known[eng]
        if kn.get(key, 0) >= val:
            return
        waits.append((key, val))
        if key[0] == "c":
            d.signal = True
        for k, v in d.clock.items():
            if kn.get(k, 0) < v:
                kn[k] = v

    def add(self, eng, fn, reads=(), writes=(), dma=False):
        op = Op()
        op.eng = eng
        op.fn = fn
        op.dma = dma
        op.signal = False
        deps = []
        for r in reads:
            if r.w is not None:
                deps.append(r.w)
        for r in writes:
            if r.w is not None:
                deps.append(r.w)
            deps.extend(r.r.values())
            deps.extend(r.rd)
        waits = []
        best = {}
        for d in deps:
            if (not dma) and (not d.dma) and d.eng == "pe" and eng == "pe":
                continue
            k, v = d.sig
            if k not in best or best[k].sig[1] < v:
                best[k] = d
        for d in best.values():
            self._wait(eng, d, waits)
        if dma:
            i = self.dma_rr
            self.dma_rr = (i + 1) % self.nd
            prev = self.dma_last[i]
            if prev is not None:
                self._wait(eng, prev, waits)
            self.dma_use[i] += 1
            op.sig = (("d", i), 16 * self.dma_use[i])
            self.dma_last[i] = op
        else:
            self.seq[eng] += 1
            op.sig = (("c", eng), self.seq[eng])
        op.waits = waits
        op.clock = dict(self.known[eng])
        op.clock[op.sig[0]] = op.sig[1]
        for r in reads:
            if dma:
                r.rd.append(op)
            else:
                r.r[eng] = op
        for r in writes:
            r.w = op
            r.r = {}
            r.rd = []
        self.streams[eng].append(op)
        if not dma:
            self.last_op[eng] = op
        return op

    def _waits_only(self, eng, deps):
        op = Op()
        op.eng = eng
        op.fn = None
        op.dma = False
        op.signal = False
        waits = []
        best = {}
        for d in deps:
            if d.eng == eng and (not d.dma) and eng == "pe":
                continue
            k, v = d.sig
            if k not in best or best[k].sig[1] < v:
                best[k] = d
        for d in best.values():
            self._wait(eng, d, waits)
        op.waits = waits
        op.sig = None
        op.clock = None
        self.streams[eng].append(op)

    def barrier(self):
        lasts = [self.last_op[e] for e in CE if self.last_op[e] is not None]
        dl = [d for d in self.dma_last if d is not None]
        for e in ALLE:
            self._waits_only(e, lasts + dl)

    def finish(self):
        dl = [d for d in self.dma_last if d is not None]
        self._waits_only("sp", dl)

    def emit(self, nc):
        valmap = {}
        for e in CE:
            cnt = 0
            vm = {}
            for op in self.streams[e]:
                if op.fn is None or op.dma:
                    continue
                if op.signal:
                    cnt += 1
                    vm[op.sig[1]] = cnt
            valmap[e] = vm
        self.nsig = {e: len(valmap[e]) for e in CE}
        with ExitStack() as st:
            sems = {}
            for e in CE:
                sems[("c", e)] = st.enter_context(nc.semaphore("s_" + e))
            for i in range(self.nd):
                sems[("d", i)] = st.enter_context(nc.semaphore("s_d%d" % i))
            block = st.enter_context(nc.Block())

            def run(ename, eobj):
                for op in self.streams[ename]:
                    for key, v in op.waits:
                        val = valmap[key[1]][v] if key[0] == "c" else v
                        eobj.wait_ge(sems[key], val)
                    if op.fn is None:
                        continue
                    ins = op.fn(eobj)
                    if op.dma:
                        ins.then_inc(sems[op.sig[0]], 16)
                    elif op.signal:
                        ins.then_inc(sems[op.sig[0]], 1)

            @block.sync
            def _(e):
                run("sp", e)

            @block.tensor
            def _(e):
                run("pe", e)

            @block.scalar
            def _(e):
                run("act", e)

            @block.vector
            def _(e):
                run("dve", e)

            @block.gpsimd
            def _(e):
                run("pool", e)


class Arena:
    def __init__(self, nc, base=17408, limit=229376):
        self.nc = nc
        self.cur = base
        self.limit = limit
        self.n = 0
        self.peak = base

    def alloc(self, shape, dtype, name="t"):
        esz = 4 if dtype == F32 else 2
        nbytes = esz
        for s in shape[1:]:
            nbytes *= s
        off = (self.cur + 63) // 64 * 64
        assert off + nbytes <= self.limit, ("SBUF overflow", name, off, nbytes, self.limit)
        self.cur = off + nbytes
        self.peak = max(self.peak, self.cur)
        self.n += 1
        return self.nc.alloc_sbuf_tensor_at("%s_%d" % (name, self.n), list(shape), dtype, offset=off)

    def mark(self):
        return self.cur

    def reset(self, m):
        self.cur = m


class TB:
    __slots__ = ("t", "r")

    def __init__(self, t):
        self.t = t
        self.r = Reg()


class Rot:
    def __init__(self, items):
        self.items = items
        self.i = 0

    def next(self):
        it = self.items[self.i]
        self.i = (self.i + 1) % len(self.items)
        return it


def rev_ap(ap2d):
    apl = [list(x) for x in ap2d.ap]
    n = apl[-1][1]
    st = apl[-1][0]
    apl[-1] = [-st, n]
    return AP(ap2d.tensor, ap2d.offset + (n - 1) * st, apl)


def gate_kchunks(m):
    n0 = (128 * m) // 160
    n1 = (128 * m + 127) // 160
    k0 = (160 * n0) // 128
    k1 = (160 * n1 + 159) // 128
    return list(range(k0, k1 + 1))


def build_nc(debug=False, phases="0ABCD", secs="ukvqgt", ntiles=9):
    nc = bass.Bass("TRN2", target_bir_lowering=False)

    def din(name, shape, dt=F32):
        return nc.dram_tensor(name, list(shape), dt, kind="ExternalInput").ap()

    x_d = din("x", [L, D])
    ctx_d = din("ctx", [NCTX, D])
    pp_d = din("pp", [128, NP])
    wmod_d = din("w_mod", [D, 6 * D])
    win_d = din("w_in", [D, 6144])
    wg_d = din("wg", [4, LW, LW])
    wol_d = din("w_out_lru", [LW, D])
    woa_d = din("w_out_attn", [D, D])
    wo_d = din("w_out", [D, D])
    wfi_d = din("w_ffn_in", [D, 2 * HID])
    wfo_d = din("w_ffn_out", [HID, D])
    cst_d = din("cst", [128, 256])
    cos_d = din("cosT", [128, L])
    sin_d = din("sinT", [128, L])
    out_d = nc.dram_tensor("out", [L, D], F32, kind="ExternalOutput").ap()
    kind = "ExternalOutput" if debug else "Internal"

    def dscr(name, shape, dt):
        return nc.dram_tensor(name, list(shape), dt, kind=kind).ap()

    XC_d = dscr("XC", [LW, T], F32)
    XCB_d = dscr("XCB", [LW, T], BF16)
    GG_d = dscr("GG", [LW, L], BF16)
    Q_d = dscr("Q", [D, L], BF16)
    TA_d = dscr("TA", [D, L], BF16)
    TBd_d = dscr("TBd", [D, L], BF16)
    KT_d = dscr("KT", [256, T], BF16)
    V_d = dscr("V", [T, 256], BF16)
    Z_d = dscr("Z", [LW, L], BF16)
    X1_d = dscr("X1", [L, D], F32)
    WFI_d = dscr("WFI", [D, 2 * HID], BF16)
    WFO_d = dscr("WFO", [HID, D], BF16)
    rXC, rXCB, rGG, rQ, rTA, rTBd, rKT, rV, rZ, rX1, rWFI, rWFO = [Reg() for _ in range(12)]

    S = Sched()
    ar = Arena(nc)
    ps2 = [nc.alloc_psum_tensor("psq%d" % i, [128, 1024], F32) for i in range(4)]
    ps = []
    for i in range(4):
        ps.append(TB(ps2[i][:, 0:512]))
        ps.append(TB(ps2[i][:, 512:1024]))

    def dma(q, out, in_, reads=(), writes=(), **kw):
        return S.add(q, lambda e: e.dma_start(out=out, in_=in_, **kw), reads, writes, dma=True)

    def act(out, in_, func, reads=(), writes=(), **kw):
        return S.add("act", lambda e: e.activation(out=out, in_=in_, func=func, **kw), reads, writes)

    def mm(out, lhsT, rhs, start, stop, reads=(), writes=()):
        return S.add("pe", lambda e: e.matmul(out, lhsT=lhsT, rhs=rhs, start=start, stop=stop), reads, writes)

    def tr(out, in_, ident, reads=(), writes=()):
        return S.add("pe", lambda e: e.transpose(out=out, in_=in_, identity=ident), reads, writes)

    def tt(eng, out, in0, in1, op, reads=(), writes=()):
        return S.add(eng, lambda e: e.tensor_tensor(out=out, in0=in0, in1=in1, op=op), reads, writes)

    def ts(eng, out, in0, s1, s2, op0, op1, reads=(), writes=()):
        return S.add(eng, lambda e: e.tensor_scalar(out=out, in0=in0, scalar1=s1, scalar2=s2, op0=op0, op1=op1), reads, writes)

    def ts1(eng, out, in0, s1, op, reads=(), writes=()):
        return S.add(eng, lambda e: e.tensor_single_scalar(out=out, in_=in0, scalar=s1, op=op), reads, writes)

    def stt(out, in0, scalar, in1, op0, op1, reads=(), writes=()):
        return S.add("dve", lambda e: e.scalar_tensor_tensor(out=out, in0=in0, scalar=scalar, in1=in1, op0=op0, op1=op1), reads, writes)

    def recip(out, in_, reads=(), writes=()):
        return S.add("dve", lambda e: e.reciprocal(out=out, in_=in_), reads, writes)

    def scan(out, d0, d1, initial, reads=(), writes=()):
        return S.add("dve", lambda e: e.tensor_tensor_scan(out=out, data0=d0, data1=d1, initial=initial, op0=ALU.mult, op1=ALU.add), reads, writes)

    def memset(eng, ap_, val, writes=()):
        return S.add(eng, lambda e: e.memset(ap_, val), (), writes)

    def cast(eng, out, in_, reads=(), writes=()):
        if eng == "act":
            return act(out, in_, AF.Copy, reads, writes)
        return S.add(eng, lambda e: e.tensor_copy(out=out, in_=in_), reads, writes)

    cst = TB(ar.alloc([128, 256], F32, "cst"))
    ident = cst.t[:, 0:128]
    perm = cst.t[:, 128:256]
    onesb = TB(ar.alloc([128, 128], BF16, "onesb"))
    onesf = TB(ar.alloc([128, 128], F32, "onesf"))
    pp = TB(ar.alloc([128, NP], F32, "pp"))
    modL = TB(ar.alloc([128, 48], F32, "modL"))
    modC = TB(ar.alloc([128, 48], F32, "modC"))
    cols = TB(ar.alloc([128, 24], F32, "cols"))
    lc = TB(ar.alloc([128, 80], F32, "lc"))
    ss = TB(ar.alloc([128, 8], F32, "ss"))
    rstd = TB(ar.alloc([128, 8], F32, "rstd"))

    dma("sp", cst.t[:, :], cst_d[:, :], writes=[cst.r])
    dma("sp", pp.t[:, :], pp_d[:, :], writes=[pp.r])
    memset("pool", onesb.t[:, :], 1.0, writes=[onesb.r])
    memset("pool", onesf.t[:, :], 1.0, writes=[onesf.r])
    persist_mark = ar.mark()

    wi = [ar.alloc([128, 6144], BF16, "wi") for _ in range(8)]
    wir = [[Reg() for _ in range(3)] for _ in range(8)]
    wi_mark = ar.mark()

    sc2 = TB(ar.alloc([128, 8, 2], F32, "sc2"))
    th = TB(ar.alloc([128, 16], F32, "th"))
    cc = pp.t[:, 176:192]
    act(th.t[:, :], cc, AF.Tanh, reads=[pp.r], writes=[th.r], scale=0.5)
    stt(th.t[:, :], th.t[:, :], 1.0, cc, ALU.add, ALU.mult, reads=[th.r, pp.r], writes=[th.r])
    for j in range(2):
        ts1("dve", sc2.t[:, :, j], th.t[:, j * 8:(j + 1) * 8], 0.5, ALU.mult, reads=[th.r], writes=[sc2.r])

    wm = [TB(ar.alloc([128, 8, 512], F32, "wm")) for _ in range(2)]
    stgR = Rot([TB(ar.alloc([128, 2048], F32, "stg")) for _ in range(3)])
    wi_jobs = [(kc, j) for j in range(3) for kc in range(8)]
    psM = ps[0]
    for cchunk in range(12):
        w = wm[cchunk % 2]
        dma("sp", w.t[:, :, :], wmod_d[:, cchunk * 512:(cchunk + 1) * 512].rearrange("(kc p) n -> p kc n", p=128),
            writes=[w.r])
        for (kc, j) in wi_jobs[cchunk * 2:cchunk * 2 + 2]:
            stg = stgR.next()
            dma("sp", stg.t[:, :], win_d[kc * 128:(kc + 1) * 128, j * 2048:(j + 1) * 2048], writes=[stg.r])
            cast("pool", wi[kc][:, j * 2048:(j + 1) * 2048], stg.t[:, :], reads=[stg.r], writes=[wir[kc][j]])
        for ol in range(4):
            oc = cchunk * 4 + ol
            for kc in range(8):
                mm(psM.t[:, oc * 2:oc * 2 + 2], w.t[:, kc, ol * 128:(ol + 1) * 128], sc2.t[:, kc, :],
                   kc == 0, kc == 7, reads=[w.r, sc2.r], writes=[psM.r])
    psM3 = psM.t[:, 0:96].rearrange("p (o j) -> p o j", j=2)
    tt("dve", modL.t[:, :], psM3[:, :, 0], pp.t[:, 0:48], ALU.add, reads=[psM.r, pp.r], writes=[modL.r])
    tt("dve", modC.t[:, :], psM3[:, :, 1], pp.t[:, 0:48], ALU.add, reads=[psM.r, pp.r], writes=[modC.r])
    stt(cols.t[:, 0:8], modL.t[:, 8:16], 1.0, pp.t[:, 48:56], ALU.add, ALU.mult, reads=[modL.r, pp.r], writes=[cols.r])
    stt(cols.t[:, 8:16], modC.t[:, 8:16], 1.0, pp.t[:, 48:56], ALU.add, ALU.mult, reads=[modC.r, pp.r], writes=[cols.r])
    stt(cols.t[:, 16:24], modL.t[:, 32:40], 1.0, pp.t[:, 56:64], ALU.add, ALU.mult, reads=[modL.r, pp.r], writes=[cols.r])
    spl = TB(ar.alloc([128, 20], F32, "spl"))
    act(spl.t[:, :], pp.t[:, 154:174], AF.Exp, reads=[pp.r], writes=[spl.r], scale=-1.0)
    act(spl.t[:, :], spl.t[:, :], AF.Ln, reads=[spl.r], writes=[spl.r], bias=1.0)
    ts1("dve", lc.t[:, 0:20], spl.t[:, :], -4.0, ALU.mult, reads=[spl.r], writes=[lc.r])
    ts1("dve", lc.t[:, 20:40], spl.t[:, :], -8.0, ALU.mult, reads=[spl.r], writes=[lc.r])
    ts1("dve", lc.t[:, 40:60], pp.t[:, 114:134], 0.5, ALU.mult, reads=[pp.r], writes=[lc.r])
    ts1("dve", lc.t[:, 60:80], pp.t[:, 134:154], 0.5, ALU.mult, reads=[pp.r], writes=[lc.r])
    for (kc, j) in wi_jobs[24:]:
        pass

    def make_row(dst, colap, colreg, scale, psb, diag):
        for fc in range(8):
            ts1("dve", diag.t[:, :], ident, colap[:, fc:fc + 1], ALU.mult, reads=[cst.r, colreg], writes=[diag.r])
            mm(psb.t[:, (fc % 4) * 128:(fc % 4 + 1) * 128], onesf.t[:, :], diag.t[:, :], True, True,
               reads=[onesf.r, diag.r], writes=[psb.r])
            if fc % 4 == 3:
                h = fc // 4
                act(dst.t[:, h * 512:(h + 1) * 512], psb.t[:, :], AF.Copy, reads=[psb.r], writes=[dst.r], scale=scale)

    if debug:
        MD_d = nc.dram_tensor("MD", [128, 96], F32, kind="ExternalOutput").ap()
        dma("sp", MD_d[:, 0:48], modL.t[:, :], reads=[modL.r])
        dma("sp", MD_d[:, 48:96], modC.t[:, :], reads=[modC.r])
    S.barrier()
    ar.reset(wi_mark)

    if "A" in phases:
        hTS = [[TB(ar.alloc([128, 512], BF16, "hT")) for _ in range(8)] for _ in range(2)]
        xt = [TB(ar.alloc([128, 1024], F32, "xt")) for _ in range(4)]
        sqj = TB(ar.alloc([128, 1024], BF16, "sqj"))
        cosS = [TB(ar.alloc([128, 512], F32, "cosb")) for _ in range(2)]
        sinS = [TB(ar.alloc([128, 512], F32, "sinb")) for _ in range(2)]
        halo = TB(ar.alloc([128, 10, 4], F32, "halo"))
        ubR = Rot([TB(ar.alloc([128, 516], F32, "ub")) for _ in range(4)])
        xcoR = Rot([TB(ar.alloc([128, 516], F32, "xco")) for _ in range(4)])
        xcbR = Rot([TB(ar.alloc([128, 516], BF16, "xcbo")) for _ in range(4)])
        gsbR = Rot([TB(ar.alloc([128, 512], F32, "gsb")) for _ in range(3)])
        g2R = Rot([TB(ar.alloc([128, 512], F32, "g2")) for _ in range(3)])
        hgoR = Rot([TB(ar.alloc([128, 512], BF16, "hgo")) for _ in range(3)])
        sqR = Rot([TB(ar.alloc([128, 512], BF16, "sq")) for _ in range(2)])
        qgR = Rot([TB(ar.alloc([128, 512], F32, "qg")) for _ in range(3)])
        sdR = Rot([TB(ar.alloc([128, 512], F32, "sd")) for _ in range(2)])
        t1R = Rot([TB(ar.alloc([128, 512], F32, "t1")) for _ in range(2)])
        t2R = Rot([TB(ar.alloc([128, 512], F32, "t2")) for _ in range(2)])
        qoR = Rot([TB(ar.alloc([128, 512], BF16, "qo")) for _ in range(3)])
        toR = Rot([TB(ar.alloc([128, 512], BF16, "to")) for _ in range(4)])
        voR = Rot([TB(ar.alloc([128, 256], BF16, "vo")) for _ in range(2)])
        psT = [ps[0], ps[1]]
        psO = Rot([ps[2], ps[3], ps[4]])
        psS, psR, psV = ps[5], ps[6], ps[7]

        def tile_geom(tti):
            is_ctx = tti == 0
            nc_ = 256 if is_ctx else 512
            col0 = 0 if is_ctx else NCTX + (tti - 1) * 512
            l0 = 0 if is_ctx else (tti - 1) * 512
            return is_ctx, nc_, col0, l0

        halo_r = [Reg() for _ in range(10)]

        def prologue_load(tti):
            is_ctx, nc_, col0, l0 = tile_geom(tti)
            nsub = nc_ // 128
            for s in range(nsub):
                src = ctx_d[s * 128:(s + 1) * 128, :] if is_ctx else x_d[l0 + s * 128:l0 + (s + 1) * 128, :]
                dma("sp", xt[s].t[:, :], src, writes=[xt[s].r])
            if not is_ctx:
                dma("sp", cosS[tti % 2].t[:, :], cos_d[:, l0:l0 + 512], writes=[cosS[tti % 2].r])
                dma("sp", sinS[tti % 2].t[:, :], sin_d[:, l0:l0 + 512], writes=[sinS[tti % 2].r])

        def prologue_stats(tti):
            is_ctx, nc_, col0, l0 = tile_geom(tti)
            nsub = nc_ // 128
            for s in range(nsub):
                act(sqj.t[:, :], xt[s].t[:, :], AF.Square, reads=[xt[s].r], writes=[sqj.r, ss.r], accum_out=ss.t[:, s:s + 1])
            act(rstd.t[:, 0:nsub], ss.t[:, 0:nsub], AF.Sqrt, reads=[ss.r], writes=[rstd.r], scale=1.0 / D, bias=EPS)
            recip(rstd.t[:, 0:nsub], rstd.t[:, 0:nsub], reads=[rstd.r], writes=[rstd.r])
            for s in range(nsub):
                ts1("dve", xt[s].t[:, :], xt[s].t[:, :], rstd.t[:, s:s + 1], ALU.mult, reads=[xt[s].r, rstd.r], writes=[xt[s].r])

        def prologue_compute(tti):
            is_ctx, nc_, col0, l0 = tile_geom(tti)
            nsub = nc_ // 128
            hT = hTS[tti % 2]
            acol = 8 if is_ctx else 0
            bmod = modC if is_ctx else modL
            for fc in range(8):
                bank = psT[fc % 2]
                for s in range(nsub):
                    tr(bank.t[:, s * 128:(s + 1) * 128], xt[s].t[:, fc * 128:(fc + 1) * 128], ident,
                       reads=[xt[s].r, cst.r], writes=[bank.r])
                act(hT[fc].t[:, 0:nc_], bank.t[:, 0:nc_], AF.Identity, reads=[bank.r, cols.r, bmod.r], writes=[hT[fc].r],
                    scale=cols.t[:, acol + fc:acol + fc + 1], bias=bmod.t[:, fc:fc + 1])

        def inproj(hT, oc, nc_):
            p = psO.next()
            third = (oc * 128) // 2048
            for kc in range(8):
                mm(p.t[:, 0:nc_], wi[kc][:, oc * 128:(oc + 1) * 128], hT[kc].t[:, 0:nc_], kc == 0, kc == 7,
                   reads=[wir[kc][third], hT[kc].r], writes=[p.r])
            return p

        def pipelined(n, stage1, stage2):
            state = {}
            if n > 0:
                state[0] = stage1(0)
            for i in range(n):
                if i + 1 < n:
                    state[i + 1] = stage1(i + 1)
                stage2(i, state.pop(i))

        def qk_stage1(hT, oc, nc_, gain_col):
            p = inproj(hT, oc, nc_)
            sq = sqR.next()
            qg = qgR.next()
            act(sq.t[:, 0:nc_], p.t[:, 0:nc_], AF.Square, reads=[p.r], writes=[sq.r])
            act(qg.t[:, 0:nc_], p.t[:, 0:nc_], AF.Identity, reads=[p.r, pp.r], writes=[qg.r], scale=gain_col)
            return sq, qg

        def qk_stage2(st, nc_, rope, cosb, sinb, dst_ap, dst_reg):
            sq, qg = st
            sd = sdR.next()
            qo = qoR.next()
            mm(psS.t[:, 0:nc_], onesb.t[:, :], sq.t[:, 0:nc_], True, True, reads=[onesb.r, sq.r], writes=[psS.r])
            if rope:
                mm(psR.t[:, 0:nc_], perm, qg.t[:, 0:nc_], True, True, reads=[cst.r, qg.r], writes=[psR.r])
            act(sd.t[:, 0:nc_], psS.t[:, 0:nc_], AF.Ln, reads=[psS.r], writes=[sd.r], scale=1.0 / 128.0, bias=EPS)
            act(sd.t[:, 0:nc_], sd.t[:, 0:nc_], AF.Exp, reads=[sd.r], writes=[sd.r], scale=-0.5)
            if rope:
                t1 = t1R.next()
                t2 = t2R.next()
                tt("pool", t1.t[:, 0:nc_], qg.t[:, 0:nc_], cosb.t[:, 0:nc_], ALU.mult, reads=[qg.r, cosb.r], writes=[t1.r])
                tt("dve", t2.t[:, 0:nc_], psR.t[:, 0:nc_], sinb.t[:, 0:nc_], ALU.mult, reads=[psR.r, sinb.r], writes=[t2.r])
                tt("pool", t1.t[:, 0:nc_], t1.t[:, 0:nc_], t2.t[:, 0:nc_], ALU.add, reads=[t1.r, t2.r], writes=[t1.r])
                tt("dve", qo.t[:, 0:nc_], t1.t[:, 0:nc_], sd.t[:, 0:nc_], ALU.mult, reads=[t1.r, sd.r], writes=[qo.r])
            else:
                tt("dve", qo.t[:, 0:nc_], qg.t[:, 0:nc_], sd.t[:, 0:nc_], ALU.mult, reads=[qg.r, sd.r], writes=[qo.r])
            dma("sp", dst_ap, qo.t[:, 0:nc_], reads=[qo.r], writes=[dst_reg])

        def amemz(ap_, writes):
            return S.add("act", lambda e: e.memzero(ap_), (), writes)

        def u_stage1(hT, tti, m):
            is_ctx, nc_, col0, l0 = tile_geom(tti)
            p = inproj(hT, m, nc_)
            ub = ubR.next()
            if is_ctx:
                amemz(ub.t[:, 0:2], [ub.r])
                amemz(ub.t[:, 258:259], [ub.r])
                act(ub.t[:, 2:258], p.t[:, 0:256], AF.Copy, reads=[p.r], writes=[ub.r])
            else:
                if tti == 1:
                    amemz(ub.t[:, 0:3], [ub.r])
                else:
                    cast("act", ub.t[:, 0:3], halo.t[:, m, 0:3], reads=[halo_r[m]], writes=[ub.r])
                if tti == 8:
                    amemz(ub.t[:, 515:516], [ub.r])
                act(ub.t[:, 3:515], p.t[:, 0:512], AF.Copy, reads=[p.r], writes=[ub.r])
                if tti != 8:
                    cast("act", halo.t[:, m, 0:3], ub.t[:, 512:515], reads=[ub.r], writes=[halo_r[m]])
            return ub

        def u_stage2(tti, m, ub):
            is_ctx, nc_, col0, l0 = tile_geom(tti)
            xo = xcoR.next()
            xb = xcbR.next()
            if is_ctx:
                k0, k1, dst0 = 0, 256, 0
            else:
                k0 = 1 if tti == 1 else 0
                k1 = 513 if tti == 8 else 512
                dst0 = NCTX + l0 - 1 + k0
            n = k1 - k0
            for tap in (3, 0, 1, 2):
                usl = ub.t[:, k0 + tap:k0 + tap + n]
                wcol = pp.t[:, 64 + tap * 10 + m:64 + tap * 10 + m + 1]
                if tap == 3:
                    act(xo.t[:, 0:n], usl, AF.Identity, reads=[ub.r, pp.r], writes=[xo.r], scale=wcol, bias=pp.t[:, 104 + m:105 + m])
                else:
                    stt(xo.t[:, 0:n], usl, wcol, xo.t[:, 0:n], ALU.mult, ALU.add, reads=[ub.r, pp.r, xo.r], writes=[xo.r])
            cast("pool", xb.t[:, 0:n], xo.t[:, 0:n], reads=[xo.r], writes=[xb.r])
            dma("sp", XC_d[m * 128:(m + 1) * 128, dst0:dst0 + n], xo.t[:, 0:n], reads=[xo.r], writes=[rXC])
            dma("sp", XCB_d[m * 128:(m + 1) * 128, dst0:dst0 + n], xb.t[:, 0:n], reads=[xb.r], writes=[rXCB])

        K0 = 0.7978845608028654
        K1 = 0.044715

        def g_stage1(hT, m, nc_):
            p = inproj(hT, 10 + m, nc_)
            gsb = gsbR.next()
            g2 = g2R.next()
            act(gsb.t[:, :], p.t[:, :], AF.Copy, reads=[p.r], writes=[gsb.r])
            act(g2.t[:, :], p.t[:, :], AF.Square, reads=[p.r], writes=[g2.r])
            ts("dve", g2.t[:, :], g2.t[:, :], K0 * K1, K0, ALU.mult, ALU.add, reads=[g2.r], writes=[g2.r])
            tt("pool", g2.t[:, :], g2.t[:, :], gsb.t[:, :], ALU.mult, reads=[g2.r, gsb.r], writes=[g2.r])
            return gsb, g2

        def g_stage2(m, l0, st):
            gsb, g2 = st
            hgo = hgoR.next()
            act(g2.t[:, :], g2.t[:, :], AF.Tanh, reads=[g2.r], writes=[g2.r])
            stt(hgo.t[:, :], g2.t[:, :], 1.0, gsb.t[:, :], ALU.add, ALU.mult, reads=[g2.r, gsb.r], writes=[hgo.r])
            dma("sp", GG_d[m * 128:(m + 1) * 128, l0:l0 + 512], hgo.t[:, :], reads=[hgo.r], writes=[rGG])

        prologue_load(0)
        prologue_stats(0)
        prologue_compute(0)
        for tti in range(ntiles):
            is_ctx, nc_, col0, l0 = tile_geom(tti)
            nsub = nc_ // 128
            hT = hTS[tti % 2]
            cosb, sinb = cosS[tti % 2], sinS[tti % 2]
            if tti + 1 < ntiles:
                prologue_load(tti + 1)
            def u2(m, ub, tti=tti):
                u_stage2(tti, m, ub)
                if m == 4 and tti + 1 < ntiles:
                    prologue_stats(tti + 1)
            pipelined(10 if "u" in secs else 0, lambda m, hT=hT, tti=tti: u_stage1(hT, tti, m), u2)
            if tti + 1 < ntiles:
                prologue_compute(tti + 1)
            pipelined(2 if "k" in secs else 0,
                      lambda hk, hT=hT, nc_=nc_: qk_stage1(hT, 28 + hk, nc_, pp.t[:, 175:176]),
                      lambda hk, st, nc_=nc_, is_ctx=is_ctx, cosb=cosb, sinb=sinb, col0=col0:
                      qk_stage2(st, nc_, not is_ctx, cosb, sinb, KT_d[hk * 128:(hk + 1) * 128, col0:col0 + nc_], rKT))
            for s in range(nsub if "v" in secs else 0):
                for kc in range(8):
                    mm(psV.t[:, 0:256], hT[kc].t[:, s * 128:(s + 1) * 128], wi[kc][:, 3840:4096], kc == 0, kc == 7,
                       reads=[hT[kc].r, wir[kc][1]], writes=[psV.r])
                vo = voR.next()
                act(vo.t[:, :], psV.t[:, 0:256], AF.Copy, reads=[psV.r], writes=[vo.r])
                dma("sp", V_d[col0 + s * 128:col0 + (s + 1) * 128, :], vo.t[:, :], reads=[vo.r], writes=[rV])
            if is_ctx:
                continue
            pipelined(8 if "q" in secs else 0,
                      lambda h, hT=hT, nc_=nc_: qk_stage1(hT, 20 + h, nc_, pp.t[:, 174:175]),
                      lambda h, st, nc_=nc_, cosb=cosb, sinb=sinb, l0=l0:
                      qk_stage2(st, nc_, True, cosb, sinb, Q_d[h * 128:(h + 1) * 128, l0:l0 + 512], rQ))
            pipelined(10 if "g" in secs else 0,
                      lambda m, hT=hT, nc_=nc_: g_stage1(hT, m, nc_),
                      lambda m, st, l0=l0: g_stage2(m, l0, st))
            for gi in range(16 if "t" in secs else 0):
                p = inproj(hT, 32 + gi, nc_)
                to = toR.next()
                act(to.t[:, :], p.t[:, :], AF.Tanh, reads=[p.r], writes=[to.r], scale=0.5)
                if gi < 8:
                    dma("sp", TA_d[gi * 128:(gi + 1) * 128, l0:l0 + 512], to.t[:, :], reads=[to.r], writes=[rTA])
                else:
                    dma("sp", TBd_d[(gi - 8) * 128:(gi - 7) * 128, l0:l0 + 512], to.t[:, :], reads=[to.r], writes=[rTBd])
        S.barrier()
    ar.reset(persist_mark)

    if "B" in phases:
        xc32 = [TB(ar.alloc([128, T], F32, "xc32")) for _ in range(2)]
        xcb = [TB(ar.alloc([128, T], BF16, "xcb")) for _ in range(4)]
        abuf = [TB(ar.alloc([128, T], F32, "abuf")) for _ in range(2)]
        a2buf = [TB(ar.alloc([128, T], F32, "a2buf")) for _ in range(2)]
        tib = [TB(ar.alloc([128, T], F32, "tib")) for _ in range(2)]
        gwst = TB(ar.alloc([128, 12, 128], F32, "gwst"))
        gws = [TB(ar.alloc([128, 12, 128], BF16, "gw")) for _ in range(2)]
        ggb = TB(ar.alloc([128, L], BF16, "gg"))
        trR = Rot([TB(ar.alloc([128, 1024], F32, "trt")) for _ in range(2)])
        pairR = Rot([0, 1, 2, 3])
        wtiles = [(0, 256)] + [(NCTX + i * 1024, 1024) for i in range(4)]
        abR = [[Reg() for _ in range(5)] for _ in range(2)]
        a2R = [[Reg() for _ in range(5)] for _ in range(2)]
        tiR = [[Reg() for _ in range(5)] for _ in range(2)]
        loaded = {"xcb": -1, "xc32": -1}

        def load_xcb(upto):
            while loaded["xcb"] < min(upto, 9):
                loaded["xcb"] += 1
                j = loaded["xcb"]
                dma("sp", xcb[j % 4].t[:, :], XCB_d[j * 128:(j + 1) * 128, :], reads=[rXCB], writes=[xcb[j % 4].r])

        def load_xc32(upto):
            while loaded["xc32"] < min(upto, 9):
                loaded["xc32"] += 1
                j = loaded["xc32"]
                dma("sp", xc32[j % 2].t[:, :], XC_d[j * 128:(j + 1) * 128, :], reads=[rXC], writes=[xc32[j % 2].r])

        def load_gw(m):
            ks = gate_kchunks(m)
            for mat in range(4):
                for ki, k in enumerate(ks):
                    dma("sp", gwst.t[:, mat * 3 + ki, :], wg_d[mat, k * 128:(k + 1) * 128, m * 128:(m + 1) * 128], writes=[gwst.r])

        load_gw(0)
        for m in range(10):
            ks = gate_kchunks(m)
            gw = gws[m % 2]
            load_xcb(m + 2)
            load_xc32(m + 1)
            for mat in range(4):
                cast("pool", gw.t[:, mat * 3:mat * 3 + len(ks), :], gwst.t[:, mat * 3:mat * 3 + len(ks), :], reads=[gwst.r], writes=[gw.r])
            if m + 1 < 10:
                load_gw(m + 1)
            if m > 0:
                dma("sp", Z_d[(m - 1) * 128:m * 128, :], ggb.t[:, :], reads=[ggb.r], writes=[rZ])
            dma("sp", ggb.t[:, :], GG_d[m * 128:(m + 1) * 128, :], reads=[rGG], writes=[ggb.r])
            x32 = xc32[m % 2]
            tbufs = {}
            for d in range(2):
                ab = abuf[d]
                a2 = a2buf[d]
                ti = tib[d if m % 2 == 0 else 1 - d]
                tbufs[d] = ti
                abr = abR[d]
                a2r = a2R[d]
                tir = tiR[d if m % 2 == 0 else 1 - d]
                ci = d * 10 + m
                for wi_, (c0, n) in enumerate(wtiles):
                    ia = pairR.next()
                    ix_ = pairR.next()
                    for gate, pi in ((0, ia), (1, ix_)):
                        for half in range((n + 511) // 512):
                            hn = min(512, n - half * 512)
                            bank = ps[2 * pi + half]
                            for ki, k in enumerate(ks):
                                mm(bank.t[:, 0:hn], gw.t[:, (d * 2 + gate) * 3 + ki, :],
                                   xcb[k % 4].t[:, c0 + half * 512:c0 + half * 512 + hn], ki == 0, ki == len(ks) - 1,
                                   reads=[gw.r, xcb[k % 4].r], writes=[bank.r])
                    nb = (n + 511) // 512
                    ra = [ps[2 * ia + h].r for h in range(nb)]
                    rx = [ps[2 * ix_ + h].r for h in range(nb)]
                    trt = trR.next()
                    act(trt.t[:, 0:n], ps2[ia][:, 0:n], AF.Tanh, reads=ra + [lc.r], writes=[trt.r],
                        scale=0.5, bias=lc.t[:, 40 + ci:41 + ci])
                    act(ab.t[:, c0:c0 + n], trt.t[:, 0:n], AF.Exp, reads=[trt.r, lc.r], writes=[abr[wi_]],
                        scale=lc.t[:, ci:ci + 1], bias=lc.t[:, ci:ci + 1])
                    act(a2.t[:, c0:c0 + n], ab.t[:, c0:c0 + n], AF.Square, reads=[abr[wi_]], writes=[a2r[wi_]])
                    act(ti.t[:, c0:c0 + n], ps2[ix_][:, 0:n], AF.Tanh, reads=rx + [lc.r], writes=[tir[wi_]],
                        scale=0.5, bias=lc.t[:, 60 + ci:61 + ci])
                act(a2.t[:, :], a2.t[:, :], AF.Sqrt, reads=a2r, writes=a2r, scale=-0.25, bias=0.25)
                stt(ti.t[:, :], ti.t[:, :], 1.0, x32.t[:, :], ALU.add, ALU.mult, reads=tir + [x32.r], writes=tir)
                tt("dve", a2.t[:, :], a2.t[:, :], ti.t[:, :], ALU.mult, reads=a2r + tir, writes=a2r)
                if d == 0:
                    scan(ti.t[:, :], ab.t[:, :], a2.t[:, :], 0.0, reads=abr + a2r, writes=tir)
                else:
                    scan(rev_ap(x32.t[:, 0:NCTX]), rev_ap(ab.t[:, 0:NCTX]), rev_ap(a2.t[:, 0:NCTX]), 0.0,
                         reads=abr + a2r + tir, writes=[x32.r])
                    scan(rev_ap(x32.t[:, NCTX:T]), rev_ap(ab.t[:, NCTX:T]), rev_ap(a2.t[:, NCTX:T]), x32.t[:, 0:1],
                         reads=abr + a2r + [x32.r], writes=[x32.r])
            hf = tbufs[0]
            hfr = tiR[0 if m % 2 == 0 else 1]
            tt("dve", hf.t[:, NCTX:T], hf.t[:, NCTX:T], x32.t[:, NCTX:T], ALU.add,
               reads=hfr + [x32.r], writes=hfr)
            stt(ggb.t[:, :], hf.t[:, NCTX:T], 0.5, ggb.t[:, :], ALU.mult, ALU.mult, reads=hfr + [ggb.r], writes=[ggb.r])
        dma("sp", Z_d[9 * 128:10 * 128, :], ggb.t[:, :], reads=[ggb.r], writes=[rZ])
        S.barrier()
        ar.reset(persist_mark)

    if "C" in phases:
        KT = [TB(ar.alloc([128, T], BF16, "KT")) for _ in range(2)]
        Vt = TB(ar.alloc([128, 34, 256], BF16, "Vt"))
        for hk in range(2):
            dma("sp", KT[hk].t[:, :], KT_d[hk * 128:(hk + 1) * 128, :], reads=[rKT], writes=[KT[hk].r])
        dma("sp", Vt.t[:, :, :], V_d.rearrange("(c p) n -> p c n", p=128), reads=[rV], writes=[Vt.r])
        woa = [TB(ar.alloc([128, D], BF16, "woa")) for _ in range(8)]
        wol = [TB(ar.alloc([128, D], BF16, "wol")) for _ in range(10)]
        wo = [TB(ar.alloc([128, D], BF16, "wo")) for _ in range(8)]
        stgC = Rot([TB(ar.alloc([128, D], F32, "stgc")) for _ in range(2)])
        qt = TB(ar.alloc([128, 8, 512], BF16, "qt"))
        zt = TB(ar.alloc([128, 10, 512], BF16, "zt"))
        tat = TB(ar.alloc([128, 8, 512], BF16, "tat"))
        tbt = TB(ar.alloc([128, 8, 512], BF16, "tbt"))
        xtS = [[TB(ar.alloc([128, D], F32, "xtc")) for _ in range(4)] for _ in range(2)]

        def loads_c(j):
            c0 = j * 512
            dma("sp", qt.t[:, :, :], Q_d[:, c0:c0 + 512].rearrange("(h p) n -> p h n", p=128), reads=[rQ], writes=[qt.r])
            dma("sp", tbt.t[:, :, :], TBd_d[:, c0:c0 + 512].rearrange("(h p) n -> p h n", p=128), reads=[rTBd], writes=[tbt.r])
            dma("sp", zt.t[:, :, :], Z_d[:, c0:c0 + 512].rearrange("(h p) n -> p h n", p=128), reads=[rZ], writes=[zt.r])
            dma("sp", tat.t[:, :, :], TA_d[:, c0:c0 + 512].rearrange("(h p) n -> p h n", p=128), reads=[rTA], writes=[tat.r])
            for s in range(4):
                dma("sp", xtS[j % 2][s].t[:, :], x_d[c0 + s * 128:c0 + (s + 1) * 128, :], writes=[xtS[j % 2][s].r])

        loads_c(0)
        for (wl, src, nk) in ((woa, woa_d, 8), (wol, wol_d, 10), (wo, wo_d, 8)):
            for kc in range(nk):
                stg = stgC.next()
                dma("sp", stg.t[:, :], src[kc * 128:(kc + 1) * 128, :], writes=[stg.r])
                cast("pool", wl[kc].t[:, :], stg.t[:, :], reads=[stg.r], writes=[wl[kc].r])
        gmh = TB(ar.alloc([128, D], F32, "gmh"))
        diag = TB(ar.alloc([128, 128], F32, "diag"))
        make_row(gmh, modL.t[:, 16:24], modL.r, 0.5, ps[7], diag)
        att = [TB(ar.alloc([128, 512], BF16, "att")) for _ in range(8)]
        mBR = Rot([TB(ar.alloc([128, 512], F32, "mB")) for _ in range(2)])
        mt = [TB(ar.alloc([128, 512], BF16, "mt")) for _ in range(8)]
        ptR = Rot([TB(ar.alloc([128, 1024], BF16, "pt")) for _ in range(3)])
        rdR = Rot([TB(ar.alloc([128, 512], F32, "rden")) for _ in range(2)])
        tmR = Rot([TB(ar.alloc([128, 512], F32, "tmpc")) for _ in range(2)])
        spR = Rot([0, 1])
        psOa = Rot([ps[4], ps[5]])
        psDa = Rot([ps[6], ps[7]])
        psP = Rot([ps[0], ps[1], ps[2], ps[3]])
        SCL = 1.0 / float(np.sqrt(128.0))
        items = [(h, pr) for h in range(8) for pr in range(17)]
        cvo = Rot([TB(ar.alloc([128, D], BF16, "cvo")) for _ in range(2)])
        cjobs = []
        for kc in range(8):
            for q0 in range(0, 2 * HID, 1024):
                q1 = min(q0 + 1024, 2 * HID)
                cjobs.append((wfi_d[kc * 128:(kc + 1) * 128, q0:q1], WFI_d[kc * 128:(kc + 1) * 128, q0:q1], q1 - q0, rWFI))
        for kc in range(22):
            cjobs.append((wfo_d[kc * 128:(kc + 1) * 128, :], WFO_d[kc * 128:(kc + 1) * 128, :], D, rWFO))
        per_tile = (len(cjobs) + 7) // 8
        for j in range(8):
            c0 = j * 512
            xt = xtS[j % 2]
            for (src, dst, n, rg) in cjobs[j * per_tile:(j + 1) * per_tile]:
                stg = stgC.next()
                co = cvo.next()
                dma("sp", stg.t[:, 0:n], src, writes=[stg.r])
                cast("pool", co.t[:, 0:n], stg.t[:, 0:n], reads=[stg.r], writes=[co.r])
                dma("sp", dst, co.t[:, 0:n], reads=[co.r], writes=[rg])
            st = {}
            cur = {}

            def emit_S(i):
                h, pr = items[i]
                hk = h // 4
                pb = spR.next()
                for half in range(2):
                    kc = pr * 2 + half
                    bank = ps[2 * pb + half]
                    mm(bank.t[:, :], KT[hk].t[:, kc * 128:(kc + 1) * 128], qt.t[:, h, :], True, True,
                       reads=[KT[hk].r, qt.r], writes=[bank.r])
                st[i] = pb

            def emit_rest(i):
                h, pr = items[i]
                hk = h // 4
                pb = st[i]
                if pr == 0:
                    cur["po"] = psOa.next()
                    cur["pd"] = psDa.next()
                po, pd = cur["po"], cur["pd"]
                pt = ptR.next()
                act(pt.t[:, :], ps2[pb][:, :], AF.Exp, reads=[ps[2 * pb].r, ps[2 * pb + 1].r], writes=[pt.r], scale=SCL)
                for half in range(2):
                    kc = pr * 2 + half
                    mm(po.t[:, :], Vt.t[:, kc, hk * 128:(hk + 1) * 128], pt.t[:, half * 512:(half + 1) * 512], kc == 0, kc == 33,
                       reads=[Vt.r, pt.r], writes=[po.r])
                    mm(pd.t[:, :], onesb.t[:, :], pt.t[:, half * 512:(half + 1) * 512], kc == 0, kc == 33,
                       reads=[onesb.r, pt.r], writes=[pd.r])
                if pr == 16:
                    rd = rdR.next()
                    recip(rd.t[:, :], pd.t[:, :], reads=[pd.r], writes=[rd.r])
                    tt("dve", att[h].t[:, :], po.t[:, :], rd.t[:, :], ALU.mult, reads=[po.r, rd.r], writes=[att[h].r])

            emit_S(0)
            emit_S(1)
            for i in range(len(items)):
                emit_rest(i)
                if i + 2 < len(items):
                    emit_S(i + 2)
            for oc in range(8):
                p = psP.next()
                for kc in range(8):
                    mm(p.t[:, :], woa[kc].t[:, oc * 128:(oc + 1) * 128], att[kc].t[:, :], kc == 0, kc == 7,
                       reads=[woa[kc].r, att[kc].r], writes=[p.r])
                mB = mBR.next()
                stt(mB.t[:, :], tbt.t[:, oc, :], 1.0, p.t[:, :], ALU.add, ALU.mult, reads=[tbt.r, p.r], writes=[mB.r])
                p = psP.next()
                for kc in range(10):
                    mm(p.t[:, :], wol[kc].t[:, oc * 128:(oc + 1) * 128], zt.t[:, kc, :], kc == 0, kc == 9,
                       reads=[wol[kc].r, zt.r], writes=[p.r])
                tm = tmR.next()
                stt(tm.t[:, :], tat.t[:, oc, :], 1.0, p.t[:, :], ALU.add, ALU.mult, reads=[tat.r, p.r], writes=[tm.r])
                tt("pool", mt[oc].t[:, :], tm.t[:, :], mB.t[:, :], ALU.add, reads=[tm.r, mB.r], writes=[mt[oc].r])
            if j + 1 < 8:
                loads_c(j + 1)
            for s in range(4):
                for fh in range(2):
                    p = psP.next()
                    for kc in range(8):
                        mm(p.t[:, :], mt[kc].t[:, s * 128:(s + 1) * 128], wo[kc].t[:, fh * 512:(fh + 1) * 512], kc == 0, kc == 7,
                           reads=[mt[kc].r, wo[kc].r], writes=[p.r])
                    tm = tmR.next()
                    tt("dve", tm.t[:, :], p.t[:, :], gmh.t[:, fh * 512:(fh + 1) * 512], ALU.mult, reads=[p.r, gmh.r], writes=[tm.r])
                    tt("pool", xt[s].t[:, fh * 512:(fh + 1) * 512], tm.t[:, :], xt[s].t[:, fh * 512:(fh + 1) * 512], ALU.add,
                       reads=[tm.r, xt[s].r], writes=[xt[s].r])
                dma("sp", X1_d[c0 + s * 128:c0 + (s + 1) * 128, :], xt[s].t[:, :], reads=[xt[s].r], writes=[rX1])
        S.barrier()
        ar.reset(persist_mark)

    if "D" in phases:
        wfi = [ar.alloc([128, 2 * HID], BF16, "wfi") for _ in range(8)]
        wfir = [[Reg() for _ in range(8)] for _ in range(8)]
        wfo = [ar.alloc([128, D], BF16, "wfo") for _ in range(22)]
        wfor = [Reg() for _ in range(22)]
        xs = [TB(ar.alloc([128, D], F32, "xs")) for _ in range(4)]

        def loads_d(j):
            c0 = j * 512
            for s in range(4):
                dma("sp", xs[s].t[:, :], X1_d[c0 + s * 128:c0 + (s + 1) * 128, :], reads=[rX1], writes=[xs[s].r])

        loads_d(0)
        cuts = [0, 768, 1536, 2176, HID]
        wgroups = []
        for gi_ in range(4):
            wgroups.append((cuts[gi_], cuts[gi_ + 1]))
            wgroups.append((HID + cuts[gi_], HID + cuts[gi_ + 1]))

        def wfi_reg(kc, col):
            for gi, (g0, g1) in enumerate(wgroups):
                if g0 <= col < g1:
                    return wfir[kc][gi]

        for gi, (g0, g1) in enumerate(wgroups):
            for kc in range(8):
                dma("sp", wfi[kc][:, g0:g1], WFI_d[kc * 128:(kc + 1) * 128, g0:g1], reads=[rWFI], writes=[wfir[kc][gi]])
        for kc in range(22):
            dma("sp", wfo[kc][:, :], WFO_d[kc * 128:(kc + 1) * 128, :], reads=[rWFO], writes=[wfor[kc]])
        gfr = TB(ar.alloc([128, D], F32, "gfr"))
        diag = TB(ar.alloc([128, 128], F32, "diagd"))
        make_row(gfr, modL.t[:, 40:48], modL.r, 1.0, ps[7], diag)
        sqj = TB(ar.alloc([128, D], BF16, "sqjd"))
        hfT = [TB(ar.alloc([128, 512], BF16, "hfT")) for _ in range(8)]
        aT = [TB(ar.alloc([128, 512], BF16, "aT")) for _ in range(22)]
        sgR = Rot([TB(ar.alloc([128, 512], F32, "sg")) for _ in range(2)])
        x1l = TB(ar.alloc([128, D], F32, "x1l"))
        tmR = Rot([TB(ar.alloc([128, 512], F32, "tmpd")) for _ in range(2)])
        psT = [ps[0], ps[1]]
        psGa = Rot([ps[2], ps[3]])
        psUp = Rot([ps[4], ps[5]])
        psF = Rot([ps[6], ps[7]])

        for j in range(8):
            c0 = j * 512
            for s in range(4):
                act(sqj.t[:, :], xs[s].t[:, :], AF.Square, reads=[xs[s].r], writes=[sqj.r, ss.r], accum_out=ss.t[:, s:s + 1])
            act(rstd.t[:, 0:4], ss.t[:, 0:4], AF.Sqrt, reads=[ss.r], writes=[rstd.r], scale=1.0 / D, bias=EPS)
            recip(rstd.t[:, 0:4], rstd.t[:, 0:4], reads=[rstd.r], writes=[rstd.r])
            for s in range(4):
                ts1("dve", xs[s].t[:, :], xs[s].t[:, :], rstd.t[:, s:s + 1], ALU.mult, reads=[xs[s].r, rstd.r], writes=[xs[s].r])
            for fc in range(8):
                bank = psT[fc % 2]
                for s in range(4):
                    tr(bank.t[:, s * 128:(s + 1) * 128], xs[s].t[:, fc * 128:(fc + 1) * 128], ident,
                       reads=[xs[s].r, cst.r], writes=[bank.r])
                act(hfT[fc].t[:, :], bank.t[:, :], AF.Identity, reads=[bank.r, cols.r, modL.r], writes=[hfT[fc].r],
                    scale=cols.t[:, 16 + fc:17 + fc], bias=modL.t[:, 24 + fc:25 + fc])
            if j + 1 < 8:
                loads_d(j + 1)
            for c in range(22):
                pg = psGa.next()
                pu = psUp.next()
                for kc in range(8):
                    mm(pg.t[:, :], wfi[kc][:, c * 128:(c + 1) * 128], hfT[kc].t[:, :], kc == 0, kc == 7,
                       reads=[wfi_reg(kc, c * 128), hfT[kc].r], writes=[pg.r])
                for kc in range(8):
                    mm(pu.t[:, :], wfi[kc][:, HID + c * 128:HID + (c + 1) * 128], hfT[kc].t[:, :], kc == 0, kc == 7,
                       reads=[wfi_reg(kc, HID + c * 128), hfT[kc].r], writes=[pu.r])
                sg = sgR.next()
                act(sg.t[:, :], pg.t[:, :], AF.Silu, reads=[pg.r], writes=[sg.r])
                tt("dve", aT[c].t[:, :], sg.t[:, :], pu.t[:, :], ALU.mult, reads=[sg.r, pu.r], writes=[aT[c].r])
            for s in range(4):
                dma("sp", x1l.t[:, :], X1_d[c0 + s * 128:c0 + (s + 1) * 128, :], reads=[rX1], writes=[x1l.r])
                for fh in range(2):
                    p = psF.next()
                    for kc in range(22):
                        mm(p.t[:, :], aT[kc].t[:, s * 128:(s + 1) * 128], wfo[kc][:, fh * 512:(fh + 1) * 512], kc == 0, kc == 21,
                           reads=[aT[kc].r, wfor[kc]], writes=[p.r])
                    tm = tmR.next()
                    tt("dve", tm.t[:, :], p.t[:, :], gfr.t[:, fh * 512:(fh + 1) * 512], ALU.mult, reads=[p.r, gfr.r], writes=[tm.r])
                    tt("pool", x1l.t[:, fh * 512:(fh + 1) * 512], tm.t[:, :], x1l.t[:, fh * 512:(fh + 1) * 512], ALU.add,
                       reads=[tm.r, x1l.r], writes=[x1l.r])
                dma("sp", out_d[c0 + s * 128:c0 + (s + 1) * 128, :], x1l.t[:, :], reads=[x1l.r])

    S.finish()
    S.emit(nc)
    nc._sched_stats = dict(nsig=S.nsig, n_ops={e: len(S.streams[e]) for e in ALLE}, sbuf_peak=ar.peak)
    return nc


def _col(v):
    v = np.asarray(v, np.float32)
    return np.ascontiguousarray(v.reshape(-1, 128).T)


def _rope_tables():
    rows = L // 64
    row = np.repeat(np.arange(rows, dtype=np.float32), 64)
    col = np.tile(np.arange(64, dtype=np.float32), rows)
    inv_freq = (np.float32(10000.0) ** (-np.arange(0, 64, 2, dtype=np.float32) / np.float32(64))).astype(np.float32)
    ang_r = (row[:, None] * inv_freq[None, :]).astype(np.float32)
    ang_c = (col[:, None] * inv_freq[None, :]).astype(np.float32)
    cosT = np.zeros((128, L), np.float32)
    sinT = np.zeros((128, L), np.float32)
    for q, (ang, sgn) in enumerate(((ang_r, -1.0), (ang_r, 1.0), (ang_c, -1.0), (ang_c, 1.0))):
        cosT[q * 32:(q + 1) * 32, :] = np.cos(ang).T
        sinT[q * 32:(q + 1) * 32, :] = sgn * np.sin(ang).T
    return cosT, sinT


def _consts():
    cst = np.zeros((128, 256), np.float32)
    cst[:, 0:128] = np.eye(128, dtype=np.float32)
    for m in range(128):
        q = m // 32
        partner = m + 32 if q in (0, 2) else m - 32
        cst[partner, 128 + m] = 1.0
    return cst


def make_in_maps(inputs):
    g = lambda k: np.asarray(inputs[k], np.float32)
    x, c, ctx, c_ctx = g("x"), g("c"), g("ctx"), g("c_ctx")
    w_mod = np.ascontiguousarray(g("w_mod")[0])
    w_in = np.ascontiguousarray(g("w_in")[0])
    lru_wa, lru_wx = g("lru_wa")[0], g("lru_wx")[0]
    wg = np.zeros((4, LW, LW), np.float32)
    for d in range(2):
        for n in range(8):
            sl = slice(160 * n, 160 * (n + 1))
            wg[d * 2 + 0, sl, sl] = lru_wa[d, n]
            wg[d * 2 + 1, sl, sl] = lru_wx[d, n]
    cosT, sinT = _rope_tables()
    cst = _consts()
    base = np.zeros((128, NP), np.float32)
    base[:, 0:48] = _col(g("b_mod")[0])
    base[:, 48:56] = _col(g("norm_mix")[0])
    base[:, 56:64] = _col(g("norm_ffn")[0])
    cw = g("conv_w")[0]
    for j in range(4):
        base[:, 64 + j * 10:64 + (j + 1) * 10] = _col(cw[j])
    base[:, 104:114] = _col(g("conv_b")[0])
    for d in range(2):
        base[:, 114 + d * 10:124 + d * 10] = _col(g("lru_ba")[0][d])
        base[:, 134 + d * 10:144 + d * 10] = _col(g("lru_bx")[0][d])
        base[:, 154 + d * 10:164 + d * 10] = _col(g("lru_lambda")[0][d])
    base[:, 174] = g("q_norm")[0]
    base[:, 175] = g("k_norm")[0]
    base[:, 184:192] = _col(c_ctx)
    shared = dict(
        w_mod=w_mod, w_in=w_in, wg=wg,
        w_out_lru=np.ascontiguousarray(g("w_out_lru")[0]),
        w_out_attn=np.ascontiguousarray(g("w_out_attn")[0]),
        w_out=np.ascontiguousarray(g("w_out")[0]),
        w_ffn_in=np.ascontiguousarray(g("w_ffn_in")[0]),
        w_ffn_out=np.ascontiguousarray(g("w_ffn_out")[0]),
        cst=cst, cosT=cosT, sinT=sinT,
    )
    maps = []
    for b in range(8):
        pp = base.copy()
        pp[:, 176:184] = _col(c[b])
        m = dict(shared)
        m["x"] = np.ascontiguousarray(x[b])
        m["ctx"] = np.ascontiguousarray(ctx[b])
        m["pp"] = pp
        maps.append(m)
    return maps


_NC_CACHE = {}


def kernel(**inputs):
    if "nc" not in _NC_CACHE:
        _NC_CACHE["nc"] = build_nc()
    nc = _NC_CACHE["nc"]
    in_maps = make_in_maps(inputs)
    res = run_bass_kernel_spmd(nc, in_maps, core_ids=list(range(8)))
    out = np.stack([np.asarray(r["out"], np.float32) for r in res.results], axis=0)
    return out
```

```python
from contextlib import ExitStack

import numpy as np
import ml_dtypes

import concourse.bass as bass
import concourse.mybir as mybir
from concourse.ap import AP
from concourse.bass_utils import run_bass_kernel_spmd

F32 = mybir.dt.float32
BF16 = mybir.dt.bfloat16
AF = mybir.ActivationFunctionType
ALU = mybir.AluOpType

CE = ("pe", "act", "dve", "pool")
ALLE = ("pe", "act", "dve", "pool", "sp")

D = 1024
L = 4096
NCTX = 256
T = NCTX + L
LW = 1280
HID = 2816
EPS = 1e-6
NP = 192


class Reg:
    __slots__ = ("w", "r", "rd")

    def __init__(self):
        self.w = None
        self.r = {}
        self.rd = []


class Op:
    __slots__ = ("eng", "fn", "dma", "signal", "sig", "waits", "clock")


class Sched:
    def __init__(self, n_dma_sems=32):
        self.streams = {e: [] for e in ALLE}
        self.seq = {e: 0 for e in CE}
        self.known = {e: {} for e in ALLE}
        self.nd = n_dma_sems
        self.dma_use = [0] * n_dma_sems
        self.dma_last = [None] * n_dma_sems
        self.dma_rr = 0
        self.last_op = {e: None for e in ALLE}

    def _wait(self, eng, d, waits):
        key, val = d.sig
        kn = self.known[eng]
        if kn.get(key, 0) >= val:
            return
        waits.append((key, val))
        if key[0] == "c":
            d.signal = True
        for k, v in d.clock.items():
            if kn.get(k, 0) < v:
                kn[k] = v

    def add(self, eng, fn, reads=(), writes=(), dma=False):
        op = Op()
        op.eng = eng
        op.fn = fn
        op.dma = dma
        op.signal = False
        deps = []
        for r in reads:
            if r.w is not None:
                deps.append(r.w)
        for r in writes:
            if r.w is not None:
                deps.append(r.w)
            deps.extend(r.r.values())
            deps.extend(r.rd)
        waits = []
        best = {}
        for d in deps:
            if (not dma) and (not d.dma) and d.eng == "pe" and eng == "pe":
                continue
            k, v = d.sig
            if k not in best or best[k].sig[1] < v:
                best[k] = d
        for d in best.values():
            self._wait(eng, d, waits)
        if dma:
            i = self.dma_rr
            self.dma_rr = (i + 1) % self.nd
            prev = self.dma_last[i]
            if prev is not None:
                self._wait(eng, prev, waits)
            self.dma_use[i] += 1
            op.sig = (("d", i), 16 * self.dma_use[i])
            self.dma_last[i] = op
        else:
            self.seq[eng] += 1
            op.sig = (("c", eng), self.seq[eng])
        op.waits = waits
        op.clock = dict(self.known[eng])
        op.clock[op.sig[0]] = op.sig[1]
        for r in reads:
            if dma:
                r.rd.append(op)
            else:
                r.r[eng] = op
        for r in writes:
            r.w = op
            r.r = {}
            r.rd = []
        self.streams[eng].append(op)
        if not dma:
            self.last_op[eng] = op
        return op

    def _waits_only(self, eng, deps):
        op = Op()
        op.eng = eng
        op.fn = None
        op.dma = False
        op.signal = False
        waits = []
        best = {}
        for d in deps:
            if d.eng == eng and (not d.dma) and eng == "pe":
                continue
            k, v = d.sig
            if k not in best or best[k].sig[1] < v:
                best[k] = d
        for d in best.values():
            self._wait(eng, d, waits)
        op.waits = waits
        op.sig = None
        op.clock = None
        self.streams[eng].append(op)

    def barrier(self):
        lasts = [self.last_op[e] for e in CE if self.last_op[e] is not None]
        dl = [d for d in self.dma_last if d is not None]
        for e in ALLE:
            self._waits_only(e, lasts + dl)

    def finish(self):
        dl = [d for d in self.dma_last if d is not None]
        self._waits_only("sp", dl)

    def emit(self, nc):
        valmap = {}
        for e in CE:
            cnt = 0
            vm = {}
            for op in self.streams[e]:
                if op.fn is None or op.dma:
                    continue
                if op.signal:
                    cnt += 1
                    vm[op.sig[1]] = cnt
            valmap[e] = vm
        self.nsig = {e: len(valmap[e]) for e in CE}
        with ExitStack() as st:
            sems = {}
            for e in CE:
                sems[("c", e)] = st.enter_context(nc.semaphore("s_" + e))
            for i in range(self.nd):
                sems[("d", i)] = st.enter_context(nc.semaphore("s_d%d" % i))
            block = st.enter_context(nc.Block())

            def run(ename, eobj):
                for op in self.streams[ename]:
                    for key, v in op.waits:
                        val = valmap[key[1]][v] if key[0] == "c" else v
                        eobj.wait_ge(sems[key], val)
                    if op.fn is None:
                        continue
                    ins = op.fn(eobj)
                    if op.dma:
                        ins.then_inc(sems[op.sig[0]], 16)
                    elif op.signal:
                        ins.then_inc(sems[op.sig[0]], 1)

            @block.sync
            def _(e):
                run("sp", e)

            @block.tensor
            def _(e):
                run("pe", e)

            @block.scalar
            def _(e):
                run("act", e)

            @block.vector
            def _(e):
                run("dve", e)

            @block.gpsimd
            def _(e):
                run("pool", e)


class Arena:
    def __init__(self, nc, base=17408, limit=229376):
        self.nc = nc
        self.cur = base
        self.limit = limit
        self.n = 0
        self.peak = base

    def alloc(self, shape, dtype, name="t"):
        esz = 4 if dtype == F32 else 2
        nbytes = esz
        for s in shape[1:]:
            nbytes *= s
        off = (self.cur + 63) // 64 * 64
        assert off + nbytes <= self.limit, ("SBUF overflow", name, off, nbytes, self.limit)
        self.cur = off + nbytes
        self.peak = max(self.peak, self.cur)
        self.n += 1
        return self.nc.alloc_sbuf_tensor_at("%s_%d" % (name, self.n), list(shape), dtype, offset=off)

    def mark(self):
        return self.cur

    def reset(self, m):
        self.cur = m


class TB:
    __slots__ = ("t", "r")

    def __init__(self, t):
        self.t = t
        self.r = Reg()


class Rot:
    def __init__(self, items):
        self.items = items
        self.i = 0

    def next(self):
        it = self.items[self.i]
        self.i = (self.i + 1) % len(self.items)
        return it


def rev_ap(ap2d):
    apl = [list(x) for x in ap2d.ap]
    n = apl[-1][1]
    st = apl[-1][0]
    apl[-1] = [-st, n]
    return AP(ap2d.tensor, ap2d.offset + (n - 1) * st, apl)


def gate_kchunks(m):
    n0 = (128 * m) // 160
    n1 = (128 * m + 127) // 160
    k0 = (160 * n0) // 128
    k1 = (160 * n1 + 159) // 128
    return list(range(k0, k1 + 1))


def build_nc(debug=False, phases="0ABCD", secs="ukvqgt", ntiles=9):
    nc = bass.Bass("TRN2", target_bir_lowering=False)

    def din(name, shape, dt=F32):
        return nc.dram_tensor(name, list(shape), dt, kind="ExternalInput").ap()

    x_d = din("x", [L, D])
    ctx_d = din("ctx", [NCTX, D])
    pp_d = din("pp", [128, NP])
    wmod_d = din("w_mod", [D, 6 * D])
    win_d = din("w_in", [D, 6144])
    wg_d = din("wg", [4, LW, LW])
    wol_d = din("w_out_lru", [LW, D])
    woa_d = din("w_out_attn", [D, D])
    wo_d = din("w_out", [D, D])
    wfi_d = din("w_ffn_in", [D, 2 * HID])
    wfo_d = din("w_ffn_out", [HID, D])
    cst_d = din("cst", [128, 256])
    cos_d = din("cosT", [128, L])
    sin_d = din("sinT", [128, L])
    out_d = nc.dram_tensor("out", [L, D], F32, kind="ExternalOutput").ap()
    kind = "ExternalOutput" if debug else "Internal"

    def dscr(name, shape, dt):
        return nc.dram_tensor(name, list(shape), dt, kind=kind).ap()

    XC_d = dscr("XC", [LW, T], F32)
    XCB_d = dscr("XCB", [LW, T], BF16)
    GG_d = dscr("GG", [LW, L], BF16)
    Q_d = dscr("Q", [D, L], BF16)
    TA_d = dscr("TA", [D, L], BF16)
    TBd_d = dscr("TBd", [D, L], BF16)
    KT_d = dscr("KT", [256, T], BF16)
    V_d = dscr("V", [T, 256], BF16)
    Z_d = dscr("Z", [LW, L], BF16)
    X1_d = dscr("X1", [L, D], F32)
    WFI_d = dscr("WFI", [D, 2 * HID], BF16)
    WFO_d = dscr("WFO", [HID, D], BF16)
    rXC, rXCB, rGG, rQ, rTA, rTBd, rKT, rV, rZ, rX1, rWFI, rWFO = [Reg() for _ in range(12)]

    S = Sched()
    ar = Arena(nc)
    ps2 = [nc.alloc_psum_tensor("psq%d" % i, [128, 1024], F32) for i in range(4)]
    ps = []
    for i in range(4):
        ps.append(TB(ps2[i][:, 0:512]))
        ps.append(TB(ps2[i][:, 512:1024]))

    def dma(q, out, in_, reads=(), writes=(), **kw):
        return S.add(q, lambda e: e.dma_start(out=out, in_=in_, **kw), reads, writes, dma=True)

    def act(out, in_, func, reads=(), writes=(), **kw):
        return S.add("act", lambda e: e.activation(out=out, in_=in_, func=func, **kw), reads, writes)

    def mm(out, lhsT, rhs, start, stop, reads=(), writes=()):
        return S.add("pe", lambda e: e.matmul(out, lhsT=lhsT, rhs=rhs, start=start, stop=stop), reads, writes)

    def tr(out, in_, ident, reads=(), writes=()):
        return S.add("pe", lambda e: e.transpose(out=out, in_=in_, identity=ident), reads, writes)

    def tt(eng, out, in0, in1, op, reads=(), writes=()):
        return S.add(eng, lambda e: e.tensor_tensor(out=out, in0=in0, in1=in1, op=op), reads, writes)

    def ts(eng, out, in0, s1, s2, op0, op1, reads=(), writes=()):
        return S.add(eng, lambda e: e.tensor_scalar(out=out, in0=in0, scalar1=s1, scalar2=s2, op0=op0, op1=op1), reads, writes)

    def ts1(eng, out, in0, s1, op, reads=(), writes=()):
        return S.add(eng, lambda e: e.tensor_single_scalar(out=out, in_=in0, scalar=s1, op=op), reads, writes)

    def stt(out, in0, scalar, in1, op0, op1, reads=(), writes=()):
        return S.add("dve", lambda e: e.scalar_tensor_tensor(out=out, in0=in0, scalar=scalar, in1=in1, op0=op0, op1=op1), reads, writes)

    def recip(out, in_, reads=(), writes=()):
        return S.add("dve", lambda e: e.reciprocal(out=out, in_=in_), reads, writes)

    def scan(out, d0, d1, initial, reads=(), writes=()):
        return S.add("dve", lambda e: e.tensor_tensor_scan(out=out, data0=d0, data1=d1, initial=initial, op0=ALU.mult, op1=ALU.add), reads, writes)

    def memset(eng, ap_, val, writes=()):
        return S.add(eng, lambda e: e.memset(ap_, val), (), writes)

    def cast(eng, out, in_, reads=(), writes=()):
        if eng == "act":
            return act(out, in_, AF.Copy, reads, writes)
        return S.add(eng, lambda e: e.tensor_copy(out=out, in_=in_), reads, writes)

    cst = TB(ar.alloc([128, 256], F32, "cst"))
    ident = cst.t[:, 0:128]
    perm = cst.t[:, 128:256]
    onesb = TB(ar.alloc([128, 128], BF16, "onesb"))
    onesf = TB(ar.alloc([128, 128], F32, "onesf"))
    pp = TB(ar.alloc([128, NP], F32, "pp"))
    modL = TB(ar.alloc([128, 48], F32, "modL"))
    modC = TB(ar.alloc([128, 48], F32, "modC"))
    cols = TB(ar.alloc([128, 24], F32, "cols"))
    lc = TB(ar.alloc([128, 80], F32, "lc"))
    ss = TB(ar.alloc([128, 8], F32, "ss"))
    rstd = TB(ar.alloc([128, 8], F32, "rstd"))

    dma("sp", cst.t[:, :], cst_d[:, :], writes=[cst.r])
    dma("sp", pp.t[:, :], pp_d[:, :], writes=[pp.r])
    memset("pool", onesb.t[:, :], 1.0, writes=[onesb.r])
    memset("pool", onesf.t[:, :], 1.0, writes=[onesf.r])
    persist_mark = ar.mark()

    wi = [ar.alloc([128, 6144], BF16, "wi") for _ in range(8)]
    wir = [[Reg() for _ in range(3)] for _ in range(8)]
    wi_mark = ar.mark()

    sc2 = TB(ar.alloc([128, 8, 2], F32, "sc2"))
    th = TB(ar.alloc([128, 16], F32, "th"))
    cc = pp.t[:, 176:192]
    act(th.t[:, :], cc, AF.Tanh, reads=[pp.r], writes=[th.r], scale=0.5)
    stt(th.t[:, :], th.t[:, :], 1.0, cc, ALU.add, ALU.mult, reads=[th.r, pp.r], writes=[th.r])
    for j in range(2):
        ts1("dve", sc2.t[:, :, j], th.t[:, j * 8:(j + 1) * 8], 0.5, ALU.mult, reads=[th.r], writes=[sc2.r])

    wm = [TB(ar.alloc([128, 8, 512], F32, "wm")) for _ in range(2)]
    stgR = Rot([TB(ar.alloc([128, 2048], F32, "stg")) for _ in range(3)])
    wi_jobs = [(kc, j) for j in range(3) for kc in range(8)]
    psM = ps[0]
    for cchunk in range(12):
        w = wm[cchunk % 2]
        dma("sp", w.t[:, :, :], wmod_d[:, cchunk * 512:(cchunk + 1) * 512].rearrange("(kc p) n -> p kc n", p=128),
            writes=[w.r])
        for (kc, j) in wi_jobs[cchunk * 2:cchunk * 2 + 2]:
            stg = stgR.next()
            dma("sp", stg.t[:, :], win_d[kc * 128:(kc + 1) * 128, j * 2048:(j + 1) * 2048], writes=[stg.r])
            cast("pool", wi[kc][:, j * 2048:(j + 1) * 2048], stg.t[:, :], reads=[stg.r], writes=[wir[kc][j]])
        for ol in range(4):
            oc = cchunk * 4 + ol
            for kc in range(8):
                mm(psM.t[:, oc * 2:oc * 2 + 2], w.t[:, kc, ol * 128:(ol + 1) * 128], sc2.t[:, kc, :],
                   kc == 0, kc == 7, reads=[w.r, sc2.r], writes=[psM.r])
    psM3 = psM.t[:, 0:96].rearrange("p (o j) -> p o j", j=2)
    tt("dve", modL.t[:, :], psM3[:, :, 0], pp.t[:, 0:48], ALU.add, reads=[psM.r, pp.r], writes=[modL.r])
    tt("dve", modC.t[:, :], psM3[:, :, 1], pp.t[:, 0:48], ALU.add, reads=[psM.r, pp.r], writes=[modC.r])
    stt(cols.t[:, 0:8], modL.t[:, 8:16], 1.0, pp.t[:, 48:56], ALU.add, ALU.mult, reads=[modL.r, pp.r], writes=[cols.r])
    stt(cols.t[:, 8:16], modC.t[:, 8:16], 1.0, pp.t[:, 48:56], ALU.add, ALU.mult, reads=[modC.r, pp.r], writes=[cols.r])
    stt(cols.t[:, 16:24], modL.t[:, 32:40], 1.0, pp.t[:, 56:64], ALU.add, ALU.mult, reads=[modL.r, pp.r], writes=[cols.r])
    spl = TB(ar.alloc([128, 20], F32, "spl"))
    act(spl.t[:, :], pp.t[:, 154:174], AF.Exp, reads=[pp.r], writes=[spl.r], scale=-1.0)
    act(spl.t[:, :], spl.t[:, :], AF.Ln, reads=[spl.r], writes=[spl.r], bias=1.0)
    ts1("dve", lc.t[:, 0:20], spl.t[:, :], -4.0, ALU.mult, reads=[spl.r], writes=[lc.r])
    ts1("dve", lc.t[:, 20:40], spl.t[:, :], -8.0, ALU.mult, reads=[spl.r], writes=[lc.r])
    ts1("dve", lc.t[:, 40:60], pp.t[:, 114:134], 0.5, ALU.mult, reads=[pp.r], writes=[lc.r])
    ts1("dve", lc.t[:, 60:80], pp.t[:, 134:154], 0.5, ALU.mult, reads=[pp.r], writes=[lc.r])
    for (kc, j) in wi_jobs[24:]:
        pass

    def make_row(dst, colap, colreg, scale, psb, diag):
        for fc in range(8):
            ts1("dve", diag.t[:, :], ident, colap[:, fc:fc + 1], ALU.mult, reads=[cst.r, colreg], writes=[diag.r])
            mm(psb.t[:, (fc % 4) * 128:(fc % 4 + 1) * 128], onesf.t[:, :], diag.t[:, :], True, True,
               reads=[onesf.r, diag.r], writes=[psb.r])
            if fc % 4 == 3:
                h = fc // 4
                act(dst.t[:, h * 512:(h + 1) * 512], psb.t[:, :], AF.Copy, reads=[psb.r], writes=[dst.r], scale=scale)

    if debug:
        MD_d = nc.dram_tensor("MD", [128, 96], F32, kind="ExternalOutput").ap()
        dma("sp", MD_d[:, 0:48], modL.t[:, :], reads=[modL.r])
        dma("sp", MD_d[:, 48:96], modC.t[:, :], reads=[modC.r])
    S.barrier()
    ar.reset(wi_mark)

    if "A" in phases:
        hTS = [[TB(ar.alloc([128, 512], BF16, "hT")) for _ in range(8)] for _ in range(2)]
        xt = [TB(ar.alloc([128, 1024], F32, "xt")) for _ in range(4)]
        sqj = TB(ar.alloc([128, 1024], BF16, "sqj"))
        cosS = [TB(ar.alloc([128, 512], F32, "cosb")) for _ in range(2)]
        sinS = [TB(ar.alloc([128, 512], F32, "sinb")) for _ in range(2)]
        halo = TB(ar.alloc([128, 10, 4], F32, "halo"))
        ubR = Rot([TB(ar.alloc([128, 516], F32, "ub")) for _ in range(4)])
        xcoR = Rot([TB(ar.alloc([128, 516], F32, "xco")) for _ in range(4)])
        xcbR = Rot([TB(ar.alloc([128, 516], BF16, "xcbo")) for _ in range(4)])
        gsbR = Rot([TB(ar.alloc([128, 512], F32, "gsb")) for _ in range(3)])
        g2R = Rot([TB(ar.alloc([128, 512], F32, "g2")) for _ in range(3)])
        hgoR = Rot([TB(ar.alloc([128, 512], BF16, "hgo")) for _ in range(3)])
        sqR = Rot([TB(ar.alloc([128, 512], BF16, "sq")) for _ in range(2)])
        qgR = Rot([TB(ar.alloc([128, 512], F32, "qg")) for _ in range(3)])
        sdR = Rot([TB(ar.alloc([128, 512], F32, "sd")) for _ in range(2)])
        t1R = Rot([TB(ar.alloc([128, 512], F32, "t1")) for _ in range(2)])
        t2R = Rot([TB(ar.alloc([128, 512], F32, "t2")) for _ in range(2)])
        qoR = Rot([TB(ar.alloc([128, 512], BF16, "qo")) for _ in range(3)])
        toR = Rot([TB(ar.alloc([128, 512], BF16, "to")) for _ in range(4)])
        voR = Rot([TB(ar.alloc([128, 256], BF16, "vo")) for _ in range(2)])
        psT = [ps[0], ps[1]]
        psO = Rot([ps[2], ps[3], ps[4]])
        psS, psR, psV = ps[5], ps[6], ps[7]

        def tile_geom(tti):
            is_ctx = tti == 0
            nc_ = 256 if is_ctx else 512
            col0 = 0 if is_ctx else NCTX + (tti - 1) * 512
            l0 = 0 if is_ctx else (tti - 1) * 512
            return is_ctx, nc_, col0, l0

        halo_r = [Reg() for _ in range(10)]

        def prologue_load(tti):
            is_ctx, nc_, col0, l0 = tile_geom(tti)
            nsub = nc_ // 128
            for s in range(nsub):
                src = ctx_d[s * 128:(s + 1) * 128, :] if is_ctx else x_d[l0 + s * 128:l0 + (s + 1) * 128, :]
                dma("sp", xt[s].t[:, :], src, writes=[xt[s].r])
            if not is_ctx:
                dma("sp", cosS[tti % 2].t[:, :], cos_d[:, l0:l0 + 512], writes=[cosS[tti % 2].r])
                dma("sp", sinS[tti % 2].t[:, :], sin_d[:, l0:l0 + 512], writes=[sinS[tti % 2].r])

        def prologue_stats(tti):
            is_ctx, nc_, col0, l0 = tile_geom(tti)
            nsub = nc_ // 128
            for s in range(nsub):
                act(sqj.t[:, :], xt[s].t[:, :], AF.Square, reads=[xt[s].r], writes=[sqj.r, ss.r], accum_out=ss.t[:, s:s + 1])
            act(rstd.t[:, 0:nsub], ss.t[:, 0:nsub], AF.Sqrt, reads=[ss.r], writes=[rstd.r], scale=1.0 / D, bias=EPS)
            recip(rstd.t[:, 0:nsub], rstd.t[:, 0:nsub], reads=[rstd.r], writes=[rstd.r])
            for s in range(nsub):
                ts1("dve", xt[s].t[:, :], xt[s].t[:, :], rstd.t[:, s:s + 1], ALU.mult, reads=[xt[s].r, rstd.r], writes=[xt[s].r])

        def prologue_compute(tti):
            is_ctx, nc_, col0, l0 = tile_geom(tti)
            nsub = nc_ // 128
            hT = hTS[tti % 2]
            acol = 8 if is_ctx else 0
            bmod = modC if is_ctx else modL
            for fc in range(8):
                bank = psT[fc % 2]
                for s in range(nsub):
                    tr(bank.t[:, s * 128:(s + 1) * 128], xt[s].t[:, fc * 128:(fc + 1) * 128], ident,
                       reads=[xt[s].r, cst.r], writes=[bank.r])
                act(hT[fc].t[:, 0:nc_], bank.t[:, 0:nc_], AF.Identity, reads=[bank.r, cols.r, bmod.r], writes=[hT[fc].r],
                    scale=cols.t[:, acol + fc:acol + fc + 1], bias=bmod.t[:, fc:fc + 1])

        def inproj(hT, oc, nc_):
            p = psO.next()
            third = (oc * 128) // 2048
            for kc in range(8):
                mm(p.t[:, 0:nc_], wi[kc][:, oc * 128:(oc + 1) * 128], hT[kc].t[:, 0:nc_], kc == 0, kc == 7,
                   reads=[wir[kc][third], hT[kc].r], writes=[p.r])
            return p

        def pipelined(n, stage1, stage2):
            state = {}
            if n > 0:
                state[0] = stage1(0)
            for i in range(n):
                if i + 1 < n:
                    state[i + 1] = stage1(i + 1)
                stage2(i, state.pop(i))

        def qk_stage1(hT, oc, nc_, gain_col):
            p = inproj(hT, oc, nc_)
            sq = sqR.next()
            qg = qgR.next()
            act(sq.t[:, 0:nc_], p.t[:, 0:nc_], AF.Square, reads=[p.r], writes=[sq.r])
            act(qg.t[:, 0:nc_], p.t[:, 0:nc_], AF.Identity, reads=[p.r, pp.r], writes=[qg.r], scale=gain_col)
            return sq, qg

        def qk_stage2(st, nc_, rope, cosb, sinb, dst_ap, dst_reg):
            sq, qg = st
            sd = sdR.next()
            qo = qoR.next()
            mm(psS.t[:, 0:nc_], onesb.t[:, :], sq.t[:, 0:nc_], True, True, reads=[onesb.r, sq.r], writes=[psS.r])
            if rope:
                mm(psR.t[:, 0:nc_], perm, qg.t[:, 0:nc_], True, True, reads=[cst.r, qg.r], writes=[psR.r])
            act(sd.t[:, 0:nc_], psS.t[:, 0:nc_], AF.Ln, reads=[psS.r], writes=[sd.r], scale=1.0 / 128.0, bias=EPS)
            act(sd.t[:, 0:nc_], sd.t[:, 0:nc_], AF.Exp, reads=[sd.r], writes=[sd.r], scale=-0.5)
            if rope:
                t1 = t1R.next()
                t2 = t2R.next()
                tt("pool", t1.t[:, 0:nc_], qg.t[:, 0:nc_], cosb.t[:, 0:nc_], ALU.mult, reads=[qg.r, cosb.r], writes=[t1.r])
                tt("dve", t2.t[:, 0:nc_], psR.t[:, 0:nc_], sinb.t[:, 0:nc_], ALU.mult, reads=[psR.r, sinb.r], writes=[t2.r])
                tt("pool", t1.t[:, 0:nc_], t1.t[:, 0:nc_], t2.t[:, 0:nc_], ALU.add, reads=[t1.r, t2.r], writes=[t1.r])
                tt("dve", qo.t[:, 0:nc_], t1.t[:, 0:nc_], sd.t[:, 0:nc_], ALU.mult, reads=[t1.r, sd.r], writes=[qo.r])
            else:
                tt("dve", qo.t[:, 0:nc_], qg.t[:, 0:nc_], sd.t[:, 0:nc_], ALU.mult, reads=[qg.r, sd.r], writes=[qo.r])
            dma("sp", dst_ap, qo.t[:, 0:nc_], reads=[qo.r], writes=[dst_reg])

        def amemz(ap_, writes):
            return S.add("act", lambda e: e.memzero(ap_), (), writes)

        def u_stage1(hT, tti, m):
            is_ctx, nc_, col0, l0 = tile_geom(tti)
            p = inproj(hT, m, nc_)
            ub = ubR.next()
            if is_ctx:
                amemz(ub.t[:, 0:2], [ub.r])
                amemz(ub.t[:, 258:259], [ub.r])
                act(ub.t[:, 2:258], p.t[:, 0:256], AF.Copy, reads=[p.r], writes=[ub.r])
            else:
                if tti == 1:
                    amemz(ub.t[:, 0:3], [ub.r])
                else:
                    cast("act", ub.t[:, 0:3], halo.t[:, m, 0:3], reads=[halo_r[m]], writes=[ub.r])
                if tti == 8:
                    amemz(ub.t[:, 515:516], [ub.r])
                act(ub.t[:, 3:515], p.t[:, 0:512], AF.Copy, reads=[p.r], writes=[ub.r])
                if tti != 8:
                    cast("act", halo.t[:, m, 0:3], ub.t[:, 512:515], reads=[ub.r], writes=[halo_r[m]])
            return ub

        def u_stage2(tti, m, ub):
            is_ctx, nc_, col0, l0 = tile_geom(tti)
            xo = xcoR.next()
            xb = xcbR.next()
            if is_ctx:
                k0, k1, dst0 = 0, 256, 0
            else:
                k0 = 1 if tti == 1 else 0
                k1 = 513 if tti == 8 else 512
                dst0 = NCTX + l0 - 1 + k0
            n = k1 - k0
            for tap in (3, 0, 1, 2):
                usl = ub.t[:, k0 + tap:k0 + tap + n]
                wcol = pp.t[:, 64 + tap * 10 + m:64 + tap * 10 + m + 1]
                if tap == 3:
                    act(xo.t[:, 0:n], usl, AF.Identity, reads=[ub.r, pp.r], writes=[xo.r], scale=wcol, bias=pp.t[:, 104 + m:105 + m])
                else:
                    stt(xo.t[:, 0:n], usl, wcol, xo.t[:, 0:n], ALU.mult, ALU.add, reads=[ub.r, pp.r, xo.r], writes=[xo.r])
            cast("pool", xb.t[:, 0:n], xo.t[:, 0:n], reads=[xo.r], writes=[xb.r])
            dma("sp", XC_d[m * 128:(m + 1) * 128, dst0:dst0 + n], xo.t[:, 0:n], reads=[xo.r], writes=[rXC])
            dma("sp", XCB_d[m * 128:(m + 1) * 128, dst0:dst0 + n], xb.t[:, 0:n], reads=[xb.r], writes=[rXCB])

        K0 = 0.7978845608028654
        K1 = 0.044715

        def g_stage1(hT, m, nc_):
            p = inproj(hT, 10 + m, nc_)
            gsb = gsbR.next()
            g2 = g2R.next()
            act(gsb.t[:, :], p.t[:, :], AF.Copy, reads=[p.r], writes=[gsb.r])
            act(g2.t[:, :], p.t[:, :], AF.Square, reads=[p.r], writes=[g2.r])
            ts("dve", g2.t[:, :], g2.t[:, :], K0 * K1, K0, ALU.mult, ALU.add, reads=[g2.r], writes=[g2.r])
            tt("pool", g2.t[:, :], g2.t[:, :], gsb.t[:, :], ALU.mult, reads=[g2.r, gsb.r], writes=[g2.r])
            return gsb, g2

        def g_stage2(m, l0, st):
            gsb, g2 = st
            hgo = hgoR.next()
            act(g2.t[:, :], g2.t[:, :], AF.Tanh, reads=[g2.r], writes=[g2.r])
            stt(hgo.t[:, :], g2.t[:, :], 1.0, gsb.t[:, :], ALU.add, ALU.mult, reads=[g2.r, gsb.r], writes=[hgo.r])
            dma("sp", GG_d[m * 128:(m + 1) * 128, l0:l0 + 512], hgo.t[:, :], reads=[hgo.r], writes=[rGG])

        prologue_load(0)
        prologue_stats(0)
        prologue_compute(0)
        for tti in range(ntiles):
            is_ctx, nc_, col0, l0 = tile_geom(tti)
            nsub = nc_ // 128
            hT = hTS[tti % 2]
            cosb, sinb = cosS[tti % 2], sinS[tti % 2]
            if tti + 1 < ntiles:
                prologue_load(tti + 1)
            def u2(m, ub, tti=tti):
                u_stage2(tti, m, ub)
                if m == 4 and tti + 1 < ntiles:
                    prologue_stats(tti + 1)
            pipelined(10 if "u" in secs else 0, lambda m, hT=hT, tti=tti: u_stage1(hT, tti, m), u2)
            if tti + 1 < ntiles:
                prologue_compute(tti + 1)
            pipelined(2 if "k" in secs else 0,
                      lambda hk, hT=hT, nc_=nc_: qk_stage1(hT, 28 + hk, nc_, pp.t[:, 175:176]),
                      lambda hk, st, nc_=nc_, is_ctx=is_ctx, cosb=cosb, sinb=sinb, col0=col0:
                      qk_stage2(st, nc_, not is_ctx, cosb, sinb, KT_d[hk * 128:(hk + 1) * 128, col0:col0 + nc_], rKT))
            for s in range(nsub if "v" in secs else 0):
                for kc in range(8):
                    mm(psV.t[:, 0:256], hT[kc].t[:, s * 128:(s + 1) * 128], wi[kc][:, 3840:4096], kc == 0, kc == 7,
                       reads=[hT[kc].r, wir[kc][1]], writes=[psV.r])
                vo = voR.next()
                act(vo.t[:, :], psV.t[:, 0:256], AF.Copy, reads=[psV.r], writes=[vo.r])
                dma("sp", V_d[col0 + s * 128:col0 + (s + 1) * 128, :], vo.t[:, :], reads=[vo.r], writes=[rV])
            if is_ctx:
                continue
            pipelined(8 if "q" in secs else 0,
                      lambda h, hT=hT, nc_=nc_: qk_stage1(hT, 20 + h, nc_, pp.t[:, 174:175]),
                      lambda h, st, nc_=nc_, cosb=cosb, sinb=sinb, l0=l0:
                      qk_stage2(st, nc_, True, cosb, sinb, Q_d[h * 128:(h + 1) * 128, l0:l0 + 512], rQ))
            pipelined(10 if "g" in secs else 0,
                      lambda m, hT=hT, nc_=nc_: g_stage1(hT, m, nc_),
                      lambda m, st, l0=l0: g_stage2(m, l0, st))
            for gi in range(16 if "t" in secs else 0):
                p = inproj(hT, 32 + gi, nc_)
                to = toR.next()
                act(to.t[:, :], p.t[:, :], AF.Tanh, reads=[p.r], writes=[to.r], scale=0.5)
                if gi < 8:
                    dma("sp", TA_d[gi * 128:(gi + 1) * 128, l0:l0 + 512], to.t[:, :], reads=[to.r], writes=[rTA])
                else:
                    dma("sp", TBd_d[(gi - 8) * 128:(gi - 7) * 128, l0:l0 + 512], to.t[:, :], reads=[to.r], writes=[rTBd])
        S.barrier()
    ar.reset(persist_mark)

    if "B" in phases:
        xc32 = [TB(ar.alloc([128, T], F32, "xc32")) for _ in range(2)]
        xcb = [TB(ar.alloc([128, T], BF16, "xcb")) for _ in range(4)]
        abuf = [TB(ar.alloc([128, T], F32, "abuf")) for _ in range(2)]
        a2buf = [TB(ar.alloc([128, T], F32, "a2buf")) for _ in range(2)]
        tib = [TB(ar.alloc([128, T], F32, "tib")) for _ in range(2)]
        gwst = TB(ar.alloc([128, 12, 128], F32, "gwst"))
        gws = [TB(ar.alloc([128, 12, 128], BF16, "gw")) for _ in range(2)]
        ggb = TB(ar.alloc([128, L], BF16, "gg"))
        trR = Rot([TB(ar.alloc([128, 1024], F32, "trt")) for _ in range(2)])
        pairR = Rot([0, 1, 2, 3])
        wtiles = [(0, 256)] + [(NCTX + i * 1024, 1024) for i in range(4)]
        abR = [[Reg() for _ in range(5)] for _ in range(2)]
        a2R = [[Reg() for _ in range(5)] for _ in range(2)]
        tiR = [[Reg() for _ in range(5)] for _ in range(2)]
        loaded = {"xcb": -1, "xc32": -1}

        def load_xcb(upto):
            while loaded["xcb"] < min(upto, 9):
                loaded["xcb"] += 1
                j = loaded["xcb"]
                dma("sp", xcb[j % 4].t[:, :], XCB_d[j * 128:(j + 1) * 128, :], reads=[rXCB], writes=[xcb[j % 4].r])

        def load_xc32(upto):
            while loaded["xc32"] < min(upto, 9):
                loaded["xc32"] += 1
                j = loaded["xc32"]
                dma("sp", xc32[j % 2].t[:, :], XC_d[j * 128:(j + 1) * 128, :], reads=[rXC], writes=[xc32[j % 2].r])

        def load_gw(m):
            ks = gate_kchunks(m)
            for mat in range(4):
                for ki, k in enumerate(ks):
                    dma("sp", gwst.t[:, mat * 3 + ki, :], wg_d[mat, k * 128:(k + 1) * 128, m * 128:(m + 1) * 128], writes=[gwst.r])

        load_gw(0)
        for m in range(10):
            ks = gate_kchunks(m)
            gw = gws[m % 2]
            if m == 0:
                load_xcb(1)
                load_xc32(0)
            load_xcb(m + 2)
            load_xc32(m + 1)
            for mat in range(4):
                cast("pool", gw.t[:, mat * 3:mat * 3 + len(ks), :], gwst.t[:, mat * 3:mat * 3 + len(ks), :], reads=[gwst.r], writes=[gw.r])
            if m + 1 < 10:
                load_gw(m + 1)
            if m > 0:
                dma("sp", Z_d[(m - 1) * 128:m * 128, :], ggb.t[:, :], reads=[ggb.r], writes=[rZ])
            dma("sp", ggb.t[:, :], GG_d[m * 128:(m + 1) * 128, :], reads=[rGG], writes=[ggb.r])
            x32 = xc32[m % 2]
            tbufs = {}
            for d in range(2):
                ab = abuf[d]
                a2 = a2buf[d]
                ti = tib[d if m % 2 == 0 else 1 - d]
                tbufs[d] = ti
                abr = abR[d]
                a2r = a2R[d]
                tir = tiR[d if m % 2 == 0 else 1 - d]
                ci = d * 10 + m
                for wi_, (c0, n) in enumerate(wtiles):
                    ia = pairR.next()
                    ix_ = pairR.next()
                    for gate, pi in ((0, ia), (1, ix_)):
                        for half in range((n + 511) // 512):
                            hn = min(512, n - half * 512)
                            bank = ps[2 * pi + half]
                            for ki, k in enumerate(ks):
                                mm(bank.t[:, 0:hn], gw.t[:, (d * 2 + gate) * 3 + ki, :],
                                   xcb[k % 4].t[:, c0 + half * 512:c0 + half * 512 + hn], ki == 0, ki == len(ks) - 1,
                                   reads=[gw.r, xcb[k % 4].r], writes=[bank.r])
                    nb = (n + 511) // 512
                    ra = [ps[2 * ia + h].r for h in range(nb)]
                    rx = [ps[2 * ix_ + h].r for h in range(nb)]
                    trt = trR.next()
                    act(trt.t[:, 0:n], ps2[ia][:, 0:n], AF.Tanh, reads=ra + [lc.r], writes=[trt.r],
                        scale=0.5, bias=lc.t[:, 40 + ci:41 + ci])
                    act(ab.t[:, c0:c0 + n], trt.t[:, 0:n], AF.Exp, reads=[trt.r, lc.r], writes=[abr[wi_]],
                        scale=lc.t[:, ci:ci + 1], bias=lc.t[:, ci:ci + 1])
                    act(a2.t[:, c0:c0 + n], ab.t[:, c0:c0 + n], AF.Square, reads=[abr[wi_]], writes=[a2r[wi_]])
                    act(ti.t[:, c0:c0 + n], ps2[ix_][:, 0:n], AF.Tanh, reads=rx + [lc.r], writes=[tir[wi_]],
                        scale=0.5, bias=lc.t[:, 60 + ci:61 + ci])
                act(a2.t[:, :], a2.t[:, :], AF.Sqrt, reads=a2r, writes=a2r, scale=-0.25, bias=0.25)
                stt(ti.t[:, :], ti.t[:, :], 1.0, x32.t[:, :], ALU.add, ALU.mult, reads=tir + [x32.r], writes=tir)
                tt("dve", a2.t[:, :], a2.t[:, :], ti.t[:, :], ALU.mult, reads=a2r + tir, writes=a2r)
                if d == 0:
                    scan(ti.t[:, :], ab.t[:, :], a2.t[:, :], 0.0, reads=abr + a2r, writes=tir)
                else:
                    scan(rev_ap(x32.t[:, 0:NCTX]), rev_ap(ab.t[:, 0:NCTX]), rev_ap(a2.t[:, 0:NCTX]), 0.0,
                         reads=abr + a2r + tir, writes=[x32.r])
                    scan(rev_ap(x32.t[:, NCTX:T]), rev_ap(ab.t[:, NCTX:T]), rev_ap(a2.t[:, NCTX:T]), x32.t[:, 0:1],
                         reads=abr + a2r + [x32.r], writes=[x32.r])
            hf = tbufs[0]
            hfr = tiR[0 if m % 2 == 0 else 1]
            tt("dve", hf.t[:, NCTX:T], hf.t[:, NCTX:T], x32.t[:, NCTX:T], ALU.add,
               reads=hfr + [x32.r], writes=hfr)
            stt(ggb.t[:, :], hf.t[:, NCTX:T], 0.5, ggb.t[:, :], ALU.mult, ALU.mult, reads=hfr + [ggb.r], writes=[ggb.r])
        dma("sp", Z_d[9 * 128:10 * 128, :], ggb.t[:, :], reads=[ggb.r], writes=[rZ])
        S.barrier()
        ar.reset(persist_mark)

    if "C" in phases:
        KT = [TB(ar.alloc([128, T], BF16, "KT")) for _ in range(2)]
        Vt = TB(ar.alloc([128, 34, 256], BF16, "Vt"))
        for hk in range(2):
            dma("sp", KT[hk].t[:, :], KT_d[hk * 128:(hk + 1) * 128, :], reads=[rKT], writes=[KT[hk].r])
        dma("sp", Vt.t[:, :, :], V_d.rearrange("(c p) n -> p c n", p=128), reads=[rV], writes=[Vt.r])
        woa = [TB(ar.alloc([128, D], BF16, "woa")) for _ in range(8)]
        wol = [TB(ar.alloc([128, D], BF16, "wol")) for _ in range(10)]
        wo = [TB(ar.alloc([128, D], BF16, "wo")) for _ in range(8)]
        stgC = Rot([TB(ar.alloc([128, D], F32, "stgc")) for _ in range(2)])
        qt = TB(ar.alloc([128, 8, 512], BF16, "qt"))
        zt = TB(ar.alloc([128, 10, 512], BF16, "zt"))
        tat = TB(ar.alloc([128, 8, 512], BF16, "tat"))
        tbt = TB(ar.alloc([128, 8, 512], BF16, "tbt"))
        xtS = [[TB(ar.alloc([128, D], F32, "xtc")) for _ in range(4)] for _ in range(2)]

        def loads_c(j):
            c0 = j * 512
            dma("sp", qt.t[:, :, :], Q_d[:, c0:c0 + 512].rearrange("(h p) n -> p h n", p=128), reads=[rQ], writes=[qt.r])
            dma("sp", tbt.t[:, :, :], TBd_d[:, c0:c0 + 512].rearrange("(h p) n -> p h n", p=128), reads=[rTBd], writes=[tbt.r])
            dma("sp", zt.t[:, :, :], Z_d[:, c0:c0 + 512].rearrange("(h p) n -> p h n", p=128), reads=[rZ], writes=[zt.r])
            dma("sp", tat.t[:, :, :], TA_d[:, c0:c0 + 512].rearrange("(h p) n -> p h n", p=128), reads=[rTA], writes=[tat.r])
            for s in range(4):
                dma("sp", xtS[j % 2][s].t[:, :], x_d[c0 + s * 128:c0 + (s + 1) * 128, :], writes=[xtS[j % 2][s].r])

        loads_c(0)
        for (wl, src, nk) in ((woa, woa_d, 8), (wol, wol_d, 10), (wo, wo_d, 8)):
            for kc in range(nk):
                stg = stgC.next()
                dma("sp", stg.t[:, :], src[kc * 128:(kc + 1) * 128, :], writes=[stg.r])
                cast("pool", wl[kc].t[:, :], stg.t[:, :], reads=[stg.r], writes=[wl[kc].r])
        gmh = TB(ar.alloc([128, D], F32, "gmh"))
        diag = TB(ar.alloc([128, 128], F32, "diag"))
        make_row(gmh, modL.t[:, 16:24], modL.r, 0.5, ps[7], diag)
        att = [TB(ar.alloc([128, 512], BF16, "att")) for _ in range(8)]
        mBR = Rot([TB(ar.alloc([128, 512], F32, "mB")) for _ in range(2)])
        mt = [TB(ar.alloc([128, 512], BF16, "mt")) for _ in range(8)]
        ptR = Rot([TB(ar.alloc([128, 1024], BF16, "pt")) for _ in range(3)])
        rdR = Rot([TB(ar.alloc([128, 512], F32, "rden")) for _ in range(2)])
        tmR = Rot([TB(ar.alloc([128, 512], F32, "tmpc")) for _ in range(2)])
        spR = Rot([0, 1])
        psOa = Rot([ps[4], ps[5]])
        psDa = Rot([ps[6], ps[7]])
        psP = Rot([ps[0], ps[1], ps[2], ps[3]])
        SCL = 1.0 / float(np.sqrt(128.0))
        items = [(h, pr) for h in range(8) for pr in range(17)]
        cvo = Rot([TB(ar.alloc([128, D], BF16, "cvo")) for _ in range(2)])
        cjobs = []
        for kc in range(8):
            for q0 in range(0, 2 * HID, 1024):
                q1 = min(q0 + 1024, 2 * HID)
                cjobs.append((wfi_d[kc * 128:(kc + 1) * 128, q0:q1], WFI_d[kc * 128:(kc + 1) * 128, q0:q1], q1 - q0, rWFI))
        for kc in range(22):
            cjobs.append((wfo_d[kc * 128:(kc + 1) * 128, :], WFO_d[kc * 128:(kc + 1) * 128, :], D, rWFO))
        per_tile = (len(cjobs) + 7) // 8
        for j in range(8):
            c0 = j * 512
            xt = xtS[j % 2]
            for (src, dst, n, rg) in cjobs[j * per_tile:(j + 1) * per_tile]:
                stg = stgC.next()
                co = cvo.next()
                dma("sp", stg.t[:, 0:n], src, writes=[stg.r])
                cast("pool", co.t[:, 0:n], stg.t[:, 0:n], reads=[stg.r], writes=[co.r])
                dma("sp", dst, co.t[:, 0:n], reads=[co.r], writes=[rg])
            st = {}
            cur = {}

            def emit_S(i):
                h, pr = items[i]
                hk = h // 4
                pb = spR.next()
                for half in range(2):
                    kc = pr * 2 + half
                    bank = ps[2 * pb + half]
                    mm(bank.t[:, :], KT[hk].t[:, kc * 128:(kc + 1) * 128], qt.t[:, h, :], True, True,
                       reads=[KT[hk].r, qt.r], writes=[bank.r])
                st[i] = pb

            def emit_rest(i):
                h, pr = items[i]
                hk = h // 4
                pb = st[i]
                if pr == 0:
                    cur["po"] = psOa.next()
                    cur["pd"] = psDa.next()
                po, pd = cur["po"], cur["pd"]
                pt = ptR.next()
                act(pt.t[:, :], ps2[pb][:, :], AF.Exp, reads=[ps[2 * pb].r, ps[2 * pb + 1].r], writes=[pt.r], scale=SCL)
                for half in range(2):
                    kc = pr * 2 + half
                    mm(po.t[:, :], Vt.t[:, kc, hk * 128:(hk + 1) * 128], pt.t[:, half * 512:(half + 1) * 512], kc == 0, kc == 33,
                       reads=[Vt.r, pt.r], writes=[po.r])
                    mm(pd.t[:, :], onesb.t[:, :], pt.t[:, half * 512:(half + 1) * 512], kc == 0, kc == 33,
                       reads=[onesb.r, pt.r], writes=[pd.r])
                if pr == 16:
                    rd = rdR.next()
                    recip(rd.t[:, :], pd.t[:, :], reads=[pd.r], writes=[rd.r])
                    tt("dve", att[h].t[:, :], po.t[:, :], rd.t[:, :], ALU.mult, reads=[po.r, rd.r], writes=[att[h].r])

            emit_S(0)
            emit_S(1)
            for i in range(len(items)):
                emit_rest(i)
                if i + 2 < len(items):
                    emit_S(i + 2)
            for oc in range(8):
                p = psP.next()
                for kc in range(8):
                    mm(p.t[:, :], woa[kc].t[:, oc * 128:(oc + 1) * 128], att[kc].t[:, :], kc == 0, kc == 7,
                       reads=[woa[kc].r, att[kc].r], writes=[p.r])
                mB = mBR.next()
                stt(mB.t[:, :], tbt.t[:, oc, :], 1.0, p.t[:, :], ALU.add, ALU.mult, reads=[tbt.r, p.r], writes=[mB.r])
                p = psP.next()
                for kc in range(10):
                    mm(p.t[:, :], wol[kc].t[:, oc * 128:(oc + 1) * 128], zt.t[:, kc, :], kc == 0, kc == 9,
                       reads=[wol[kc].r, zt.r], writes=[p.r])
                tm = tmR.next()
                stt(tm.t[:, :], tat.t[:, oc, :], 1.0, p.t[:, :], ALU.add, ALU.mult, reads=[tat.r, p.r], writes=[tm.r])
                tt("pool", mt[oc].t[:, :], tm.t[:, :], mB.t[:, :], ALU.add, reads=[tm.r, mB.r], writes=[mt[oc].r])
            if j + 1 < 8:
                loads_c(j + 1)
            for s in range(4):
                for fh in range(2):
                    p = psP.next()
                    for kc in range(8):
                        mm(p.t[:, :], mt[kc].t[:, s * 128:(s + 1) * 128], wo[kc].t[:, fh * 512:(fh + 1) * 512], kc == 0, kc == 7,
                           reads=[mt[kc].r, wo[kc].r], writes=[p.r])
                    tm = tmR.next()
                    tt("dve", tm.t[:, :], p.t[:, :], gmh.t[:, fh * 512:(fh + 1) * 512], ALU.mult, reads=[p.r, gmh.r], writes=[tm.r])
                    tt("pool", xt[s].t[:, fh * 512:(fh + 1) * 512], tm.t[:, :], xt[s].t[:, fh * 512:(fh + 1) * 512], ALU.add,
                       reads=[tm.r, xt[s].r], writes=[xt[s].r])
                dma("sp", X1_d[c0 + s * 128:c0 + (s + 1) * 128, :], xt[s].t[:, :], reads=[xt[s].r], writes=[rX1])
        S.barrier()
        ar.reset(persist_mark)

    if "D" in phases:
        wfi = [ar.alloc([128, 2 * HID], BF16, "wfi") for _ in range(8)]
        wfir = [[Reg() for _ in range(8)] for _ in range(8)]
        wfo = [ar.alloc([128, D], BF16, "wfo") for _ in range(22)]
        wfor = [Reg() for _ in range(22)]
        xs = [TB(ar.alloc([128, D], F32, "xs")) for _ in range(4)]

        def loads_d(j):
            c0 = j * 512
            for s in range(4):
                dma("sp", xs[s].t[:, :], X1_d[c0 + s * 128:c0 + (s + 1) * 128, :], reads=[rX1], writes=[xs[s].r])

        loads_d(0)
        cuts = [0, 768, 1536, 2176, HID]
        wgroups = []
        for gi_ in range(4):
            wgroups.append((cuts[gi_], cuts[gi_ + 1]))
            wgroups.append((HID + cuts[gi_], HID + cuts[gi_ + 1]))

        def wfi_reg(kc, col):
            for gi, (g0, g1) in enumerate(wgroups):
                if g0 <= col < g1:
                    return wfir[kc][gi]

        for gi, (g0, g1) in enumerate(wgroups):
            for kc in range(8):
                dma("sp", wfi[kc][:, g0:g1], WFI_d[kc * 128:(kc + 1) * 128, g0:g1], reads=[rWFI], writes=[wfir[kc][gi]])
        for kc in range(22):
            dma("sp", wfo[kc][:, :], WFO_d[kc * 128:(kc + 1) * 128, :], reads=[rWFO], writes=[wfor[kc]])
        gfr = TB(ar.alloc([128, D], F32, "gfr"))
        diag = TB(ar.alloc([128, 128], F32, "diagd"))
        make_row(gfr, modL.t[:, 40:48], modL.r, 1.0, ps[7], diag)
        sqj = TB(ar.alloc([128, D], BF16, "sqjd"))
        hfT = [TB(ar.alloc([128, 512], BF16, "hfT")) for _ in range(8)]
        aT = [TB(ar.alloc([128, 512], BF16, "aT")) for _ in range(22)]
        sgR = Rot([TB(ar.alloc([128, 512], F32, "sg")) for _ in range(2)])
        x1l = TB(ar.alloc([128, D], F32, "x1l"))
        tmR = Rot([TB(ar.alloc([128, 512], F32, "tmpd")) for _ in range(2)])
        psT = [ps[0], ps[1]]
        psGa = Rot([ps[2], ps[3]])
        psUp = Rot([ps[4], ps[5]])
        psF = Rot([ps[6], ps[7]])

        for j in range(8):
            c0 = j * 512
            for s in range(4):
                act(sqj.t[:, :], xs[s].t[:, :], AF.Square, reads=[xs[s].r], writes=[sqj.r, ss.r], accum_out=ss.t[:, s:s + 1])
            act(rstd.t[:, 0:4], ss.t[:, 0:4], AF.Sqrt, reads=[ss.r], writes=[rstd.r], scale=1.0 / D, bias=EPS)
            recip(rstd.t[:, 0:4], rstd.t[:, 0:4], reads=[rstd.r], writes=[rstd.r])
            for s in range(4):
                ts1("dve", xs[s].t[:, :], xs[s].t[:, :], rstd.t[:, s:s + 1], ALU.mult, reads=[xs[s].r, rstd.r], writes=[xs[s].r])
            for fc in range(8):
                bank = psT[fc % 2]
                for s in range(4):
                    tr(bank.t[:, s * 128:(s + 1) * 128], xs[s].t[:, fc * 128:(fc + 1) * 128], ident,
                       reads=[xs[s].r, cst.r], writes=[bank.r])
                act(hfT[fc].t[:, :], bank.t[:, :], AF.Identity, reads=[bank.r, cols.r, modL.r], writes=[hfT[fc].r],
                    scale=cols.t[:, 16 + fc:17 + fc], bias=modL.t[:, 24 + fc:25 + fc])
            if j + 1 < 8:
                loads_d(j + 1)
            for c in range(22):
                pg = psGa.next()
                pu = psUp.next()
                for kc in range(8):
                    mm(pg.t[:, :], wfi[kc][:, c * 128:(c + 1) * 128], hfT[kc].t[:, :], kc == 0, kc == 7,
                       reads=[wfi_reg(kc, c * 128), hfT[kc].r], writes=[pg.r])
                for kc in range(8):
                    mm(pu.t[:, :], wfi[kc][:, HID + c * 128:HID + (c + 1) * 128], hfT[kc].t[:, :], kc == 0, kc == 7,
                       reads=[wfi_reg(kc, HID + c * 128), hfT[kc].r], writes=[pu.r])
                sg = sgR.next()
                act(sg.t[:, :], pg.t[:, :], AF.Silu, reads=[pg.r], writes=[sg.r])
                tt("dve", aT[c].t[:, :], sg.t[:, :], pu.t[:, :], ALU.mult, reads=[sg.r, pu.r], writes=[aT[c].r])
            for s in range(4):
                dma("sp", x1l.t[:, :], X1_d[c0 + s * 128:c0 + (s + 1) * 128, :], reads=[rX1], writes=[x1l.r])
                for fh in range(2):
                    p = psF.next()
                    for kc in range(22):
                        mm(p.t[:, :], aT[kc].t[:, s * 128:(s + 1) * 128], wfo[kc][:, fh * 512:(fh + 1) * 512], kc == 0, kc == 21,
                           reads=[aT[kc].r, wfor[kc]], writes=[p.r])
                    tm = tmR.next()
                    tt("dve", tm.t[:, :], p.t[:, :], gfr.t[:, fh * 512:(fh + 1) * 512], ALU.mult, reads=[p.r, gfr.r], writes=[tm.r])
                    tt("pool", x1l.t[:, fh * 512:(fh + 1) * 512], tm.t[:, :], x1l.t[:, fh * 512:(fh + 1) * 512], ALU.add,
                       reads=[tm.r, x1l.r], writes=[x1l.r])
                dma("sp", out_d[c0 + s * 128:c0 + (s + 1) * 128, :], x1l.t[:, :], reads=[x1l.r])

    S.finish()
    S.emit(nc)
    nc._sched_stats = dict(nsig=S.nsig, n_ops={e: len(S.streams[e]) for e in ALLE}, sbuf_peak=ar.peak)
    return nc


def _col(v):
    v = np.asarray(v, np.float32)
    return np.ascontiguousarray(v.reshape(-1, 128).T)


def _rope_tables():
    rows = L // 64
    row = np.repeat(np.arange(rows, dtype=np.float32), 64)
    col = np.tile(np.arange(64, dtype=np.float32), rows)
    inv_freq = (np.float32(10000.0) ** (-np.arange(0, 64, 2, dtype=np.float32) / np.float32(64))).astype(np.float32)
    ang_r = (row[:, None] * inv_freq[None, :]).astype(np.float32)
    ang_c = (col[:, None] * inv_freq[None, :]).astype(np.float32)
    cosT = np.zeros((128, L), np.float32)
    sinT = np.zeros((128, L), np.float32)
    for q, (ang, sgn) in enumerate(((ang_r, -1.0), (ang_r, 1.0), (ang_c, -1.0), (ang_c, 1.0))):
        cosT[q * 32:(q + 1) * 32, :] = np.cos(ang).T
        sinT[q * 32:(q + 1) * 32, :] = sgn * np.sin(ang).T
    return cosT, sinT


def _consts():
    cst = np.zeros((128, 256), np.float32)
    cst[:, 0:128] = np.eye(128, dtype=np.float32)
    for m in range(128):
        q = m // 32
        partner = m + 32 if q in (0, 2) else m - 32
        cst[partner, 128 + m] = 1.0
    return cst


def make_in_maps(inputs):
    g = lambda k: np.asarray(inputs[k], np.float32)
    x, c, ctx, c_ctx = g("x"), g("c"), g("ctx"), g("c_ctx")
    w_mod = np.ascontiguousarray(g("w_mod")[0])
    w_in = np.ascontiguousarray(g("w_in")[0])
    lru_wa, lru_wx = g("lru_wa")[0], g("lru_wx")[0]
    wg = np.zeros((4, LW, LW), np.float32)
    for d in range(2):
        for n in range(8):
            sl = slice(160 * n, 160 * (n + 1))
            wg[d * 2 + 0, sl, sl] = lru_wa[d, n]
            wg[d * 2 + 1, sl, sl] = lru_wx[d, n]
    cosT, sinT = _rope_tables()
    cst = _consts()
    base = np.zeros((128, NP), np.float32)
    base[:, 0:48] = _col(g("b_mod")[0])
    base[:, 48:56] = _col(g("norm_mix")[0])
    base[:, 56:64] = _col(g("norm_ffn")[0])
    cw = g("conv_w")[0]
    for j in range(4):
        base[:, 64 + j * 10:64 + (j + 1) * 10] = _col(cw[j])
    base[:, 104:114] = _col(g("conv_b")[0])
    for d in range(2):
        base[:, 114 + d * 10:124 + d * 10] = _col(g("lru_ba")[0][d])
        base[:, 134 + d * 10:144 + d * 10] = _col(g("lru_bx")[0][d])
        base[:, 154 + d * 10:164 + d * 10] = _col(g("lru_lambda")[0][d])
    base[:, 174] = g("q_norm")[0]
    base[:, 175] = g("k_norm")[0]
    base[:, 184:192] = _col(c_ctx)
    shared = dict(
        w_mod=w_mod, w_in=w_in, wg=wg,
        w_out_lru=np.ascontiguousarray(g("w_out_lru")[0]),
        w_out_attn=np.ascontiguousarray(g("w_out_attn")[0]),
        w_out=np.ascontiguousarray(g("w_out")[0]),
        w_ffn_in=np.ascontiguousarray(g("w_ffn_in")[0]),
        w_ffn_out=np.ascontiguousarray(g("w_ffn_out")[0]),
        cst=cst, cosT=cosT, sinT=sinT,
    )
    maps = []
    for b in range(8):
        pp = base.copy()
        pp[:, 176:184] = _col(c[b])
        m = dict(shared)
        m["x"] = np.ascontiguousarray(x[b])
        m["ctx"] = np.ascontiguousarray(ctx[b])
        m["pp"] = pp
        maps.append(m)
    return maps


_NC_CACHE = {}


def kernel(**inputs):
    if "nc" not in _NC_CACHE:
        _NC_CACHE["nc"] = build_nc()
    nc = _NC_CACHE["nc"]
    in_maps = make_in_maps(inputs)
    res = run_bass_kernel_spmd(nc, in_maps, core_ids=list(range(8)))
    out = np.stack([np.asarray(r["out"], np.float32) for r in res.results], axis=0)
    return out
```

```python
from contextlib import ExitStack

import numpy as np
import ml_dtypes

import concourse.bass as bass
import concourse.mybir as mybir
from concourse.ap import AP
from concourse.bass_utils import run_bass_kernel_spmd

F32 = mybir.dt.float32
BF16 = mybir.dt.bfloat16
AF = mybir.ActivationFunctionType
ALU = mybir.AluOpType

CE = ("pe", "act", "dve", "pool")
ALLE = ("pe", "act", "dve", "pool", "sp")

D = 1024
L = 4096
NCTX = 256
T = NCTX + L
LW = 1280
HID = 2816
EPS = 1e-6
NP = 192


class Reg:
    __slots__ = ("w", "r", "rd")

    def __init__(self):
        self.w = None
        self.r = {}
        self.rd = []


class Op:
    __slots__ = ("eng", "fn", "dma", "signal", "sig", "waits", "clock")


class Sched:
    def __init__(self, n_dma_sems=32):
        self.streams = {e: [] for e in ALLE}
        self.seq = {e: 0 for e in CE}
        self.known = {e: {} for e in ALLE}
        self.nd = n_dma_sems
        self.dma_use = [0] * n_dma_sems
        self.dma_last = [None] * n_dma_sems
        self.dma_rr = 0
        self.last_op = {e: None for e in ALLE}

    def _wait(self, eng, d, waits):
        key, val = d.sig
        kn = self.known[eng]
        if kn.get(key, 0) >= val:
            return
        waits.append((key, val))
        if key[0] == "c":
            d.signal = True
        for k, v in d.clock.items():
            if kn.get(k, 0) < v:
                kn[k] = v

    def add(self, eng, fn, reads=(), writes=(), dma=False):
        op = Op()
        op.eng = eng
        op.fn = fn
        op.dma = dma
        op.signal = False
        deps = []
        for r in reads:
            if r.w is not None:
                deps.append(r.w)
        for r in writes:
            if r.w is not None:
                deps.append(r.w)
            deps.extend(r.r.values())
            deps.extend(r.rd)
        waits = []
        best = {}
        for d in deps:
            if (not dma) and (not d.dma) and d.eng == "pe" and eng == "pe":
                continue
            k, v = d.sig
            if k not in best or best[k].sig[1] < v:
                best[k] = d
        for d in best.values():
            self._wait(eng, d, waits)
        if dma:
            i = self.dma_rr
            self.dma_rr = (i + 1) % self.nd
            prev = self.dma_last[i]
            if prev is not None:
                self._wait(eng, prev, waits)
            self.dma_use[i] += 1
            op.sig = (("d", i), 16 * self.dma_use[i])
            self.dma_last[i] = op
        else:
            self.seq[eng] += 1
            op.sig = (("c", eng), self.seq[eng])
        op.waits = waits
        op.clock = dict(self.known[eng])
        op.clock[op.sig[0]] = op.sig[1]
        for r in reads:
            if dma:
                r.rd.append(op)
            else:
                r.r[eng] = op
        for r in writes:
            r.w = op
            r.r = {}
            r.rd = []
        self.streams[eng].append(op)
        if not dma:
            self.last_op[eng] = op
        return op

    def _waits_only(self, eng, deps):
        op = Op()
        op.eng = eng
        op.fn = None
        op.dma = False
        op.signal = False
        waits = []
        best = {}
        for d in deps:
            if d.eng == eng and (not d.dma) and eng == "pe":
                continue
            k, v = d.sig
            if k not in best or best[k].sig[1] < v:
                best[k] = d
        for d in best.values():
            self._wait(eng, d, waits)
        op.waits = waits
        op.sig = None
        op.clock = None
        self.streams[eng].append(op)

    def barrier(self):
        lasts = [self.last_op[e] for e in CE if self.last_op[e] is not None]
        dl = [d for d in self.dma_last if d is not None]
        for e in ALLE:
            self._waits_only(e, lasts + dl)

    def finish(self):
        dl = [d for d in self.dma_last if d is not None]
        self._waits_only("sp", dl)

    def emit(self, nc):
        valmap = {}
        for e in CE:
            cnt = 0
            vm = {}
            for op in self.streams[e]:
                if op.fn is None or op.dma:
                    continue
                if op.signal:
                    cnt += 1
                    vm[op.sig[1]] = cnt
            valmap[e] = vm
        self.nsig = {e: len(valmap[e]) for e in CE}
        with ExitStack() as st:
            sems = {}
            for e in CE:
                sems[("c", e)] = st.enter_context(nc.semaphore("s_" + e))
            for i in range(self.nd):
                sems[("d", i)] = st.enter_context(nc.semaphore("s_d%d" % i))
            block = st.enter_context(nc.Block())

            def run(ename, eobj):
                for op in self.streams[ename]:
                    for key, v in op.waits:
                        val = valmap[key[1]][v] if key[0] == "c" else v
                        eobj.wait_ge(sems[key], val)
                    if op.fn is None:
                        continue
                    ins = op.fn(eobj)
                    if op.dma:
                        ins.then_inc(sems[op.sig[0]], 16)
                    elif op.signal:
                        ins.then_inc(sems[op.sig[0]], 1)

            @block.sync
            def _(e):
                run("sp", e)

            @block.tensor
            def _(e):
                run("pe", e)

            @block.scalar
            def _(e):
                run("act", e)

            @block.vector
            def _(e):
                run("dve", e)

            @block.gpsimd
            def _(e):
                run("pool", e)


class Arena:
    def __init__(self, nc, base=17408, limit=229376):
        self.nc = nc
        self.cur = base
        self.limit = limit
        self.n = 0
        self.peak = base

    def alloc(self, shape, dtype, name="t"):
        esz = 4 if dtype == F32 else 2
        nbytes = esz
        for s in shape[1:]:
            nbytes *= s
        off = (self.cur + 63) // 64 * 64
        assert off + nbytes <= self.limit, ("SBUF overflow", name, off, nbytes, self.limit)
        self.cur = off + nbytes
        self.peak = max(self.peak, self.cur)
        self.n += 1
        return self.nc.alloc_sbuf_tensor_at("%s_%d" % (name, self.n), list(shape), dtype, offset=off)

    def mark(self):
        return self.cur

    def reset(self, m):
        self.cur = m


class TB:
    __slots__ = ("t", "r")

    def __init__(self, t):
        self.t = t
        self.r = Reg()


class Rot:
    def __init__(self, items):
        self.items = items
        self.i = 0

    def next(self):
        it = self.items[self.i]
        self.i = (self.i + 1) % len(self.items)
        return it


def rev_ap(ap2d):
    apl = [list(x) for x in ap2d.ap]
    n = apl[-1][1]
    st = apl[-1][0]
    apl[-1] = [-st, n]
    return AP(ap2d.tensor, ap2d.offset + (n - 1) * st, apl)


def gate_kchunks(m):
    n0 = (128 * m) // 160
    n1 = (128 * m + 127) // 160
    k0 = (160 * n0) // 128
    k1 = (160 * n1 + 159) // 128
    return list(range(k0, k1 + 1))


def build_nc(debug=False, phases="0ABCD", secs="ukvqgt", ntiles=9):
    nc = bass.Bass("TRN2", target_bir_lowering=False)

    def din(name, shape, dt=F32):
        return nc.dram_tensor(name, list(shape), dt, kind="ExternalInput").ap()

    x_d = din("x", [L, D])
    ctx_d = din("ctx", [NCTX, D])
    pp_d = din("pp", [128, NP])
    wmod_d = din("w_mod", [D, 6 * D])
    win_d = din("w_in", [D, 6144])
    wg_d = din("wg", [4, LW, LW])
    wol_d = din("w_out_lru", [LW, D])
    woa_d = din("w_out_attn", [D, D])
    wo_d = din("w_out", [D, D])
    wfi_d = din("w_ffn_in", [D, 2 * HID])
    wfo_d = din("w_ffn_out", [HID, D])
    cst_d = din("cst", [128, 256])
    cos_d = din("cosT", [128, L])
    sin_d = din("sinT", [128, L])
    out_d = nc.dram_tensor("out", [L, D], F32, kind="ExternalOutput").ap()
    kind = "ExternalOutput" if debug else "Internal"

    def dscr(name, shape, dt):
        return nc.dram_tensor(name, list(shape), dt, kind=kind).ap()

    XC_d = dscr("XC", [LW, T], F32)
    XCB_d = dscr("XCB", [LW, T], BF16)
    GG_d = dscr("GG", [LW, L], BF16)
    Q_d = dscr("Q", [D, L], BF16)
    TA_d = dscr("TA", [D, L], BF16)
    TBd_d = dscr("TBd", [D, L], BF16)
    KT_d = dscr("KT", [256, T], BF16)
    V_d = dscr("V", [T, 256], BF16)
    Z_d = dscr("Z", [LW, L], BF16)
    X1_d = dscr("X1", [L, D], F32)
    WFI_d = dscr("WFI", [D, 2 * HID], BF16)
    WFO_d = dscr("WFO", [HID, D], BF16)
    rXC, rXCB, rGG, rQ, rTA, rTBd, rKT, rV, rZ, rX1, rWFI, rWFO = [Reg() for _ in range(12)]

    S = Sched()
    ar = Arena(nc)
    ps2 = [nc.alloc_psum_tensor("psq%d" % i, [128, 1024], F32) for i in range(4)]
    ps = []
    for i in range(4):
        ps.append(TB(ps2[i][:, 0:512]))
        ps.append(TB(ps2[i][:, 512:1024]))

    def dma(q, out, in_, reads=(), writes=(), **kw):
        return S.add(q, lambda e: e.dma_start(out=out, in_=in_, **kw), reads, writes, dma=True)

    def act(out, in_, func, reads=(), writes=(), **kw):
        return S.add("act", lambda e: e.activation(out=out, in_=in_, func=func, **kw), reads, writes)

    def mm(out, lhsT, rhs, start, stop, reads=(), writes=()):
        return S.add("pe", lambda e: e.matmul(out, lhsT=lhsT, rhs=rhs, start=start, stop=stop), reads, writes)

    def tr(out, in_, ident, reads=(), writes=()):
        return S.add("pe", lambda e: e.transpose(out=out, in_=in_, identity=ident), reads, writes)

    def tt(eng, out, in0, in1, op, reads=(), writes=()):
        return S.add(eng, lambda e: e.tensor_tensor(out=out, in0=in0, in1=in1, op=op), reads, writes)

    def ts(eng, out, in0, s1, s2, op0, op1, reads=(), writes=()):
        return S.add(eng, lambda e: e.tensor_scalar(out=out, in0=in0, scalar1=s1, scalar2=s2, op0=op0, op1=op1), reads, writes)

    def ts1(eng, out, in0, s1, op, reads=(), writes=()):
        return S.add(eng, lambda e: e.tensor_single_scalar(out=out, in_=in0, scalar=s1, op=op), reads, writes)

    def stt(out, in0, scalar, in1, op0, op1, reads=(), writes=()):
        return S.add("dve", lambda e: e.scalar_tensor_tensor(out=out, in0=in0, scalar=scalar, in1=in1, op0=op0, op1=op1), reads, writes)

    def recip(out, in_, reads=(), writes=()):
        return S.add("dve", lambda e: e.reciprocal(out=out, in_=in_), reads, writes)

    def scan(out, d0, d1, initial, reads=(), writes=()):
        return S.add("dve", lambda e: e.tensor_tensor_scan(out=out, data0=d0, data1=d1, initial=initial, op0=ALU.mult, op1=ALU.add), reads, writes)

    def memset(eng, ap_, val, writes=()):
        return S.add(eng, lambda e: e.memset(ap_, val), (), writes)

    def cast(eng, out, in_, reads=(), writes=()):
        if eng == "act":
            return act(out, in_, AF.Copy, reads, writes)
        return S.add(eng, lambda e: e.tensor_copy(out=out, in_=in_), reads, writes)

    cst = TB(ar.alloc([128, 256], F32, "cst"))
    ident = cst.t[:, 0:128]
    perm = cst.t[:, 128:256]
    onesb = TB(ar.alloc([128, 128], BF16, "onesb"))
    onesf = TB(ar.alloc([128, 128], F32, "onesf"))
    pp = TB(ar.alloc([128, NP], F32, "pp"))
    modL = TB(ar.alloc([128, 48], F32, "modL"))
    modC = TB(ar.alloc([128, 48], F32, "modC"))
    cols = TB(ar.alloc([128, 24], F32, "cols"))
    lc = TB(ar.alloc([128, 80], F32, "lc"))
    ss = TB(ar.alloc([128, 8], F32, "ss"))
    rstd = TB(ar.alloc([128, 8], F32, "rstd"))

    dma("sp", cst.t[:, :], cst_d[:, :], writes=[cst.r])
    dma("sp", pp.t[:, :], pp_d[:, :], writes=[pp.r])
    memset("pool", onesb.t[:, :], 1.0, writes=[onesb.r])
    memset("pool", onesf.t[:, :], 1.0, writes=[onesf.r])
    persist_mark = ar.mark()

    wi = [ar.alloc([128, 6144], BF16, "wi") for _ in range(8)]
    wir = [[Reg() for _ in range(3)] for _ in range(8)]
    wi_mark = ar.mark()

    sc2 = TB(ar.alloc([128, 8, 2], F32, "sc2"))
    th = TB(ar.alloc([128, 16], F32, "th"))
    cc = pp.t[:, 176:192]
    act(th.t[:, :], cc, AF.Tanh, reads=[pp.r], writes=[th.r], scale=0.5)
    stt(th.t[:, :], th.t[:, :], 1.0, cc, ALU.add, ALU.mult, reads=[th.r, pp.r], writes=[th.r])
    for j in range(2):
        ts1("dve", sc2.t[:, :, j], th.t[:, j * 8:(j + 1) * 8], 0.5, ALU.mult, reads=[th.r], writes=[sc2.r])

    wm = [TB(ar.alloc([128, 8, 512], F32, "wm")) for _ in range(2)]
    stgR = Rot([TB(ar.alloc([128, 2048], F32, "stg")) for _ in range(3)])
    wi_jobs = [(kc, j) for j in range(3) for kc in range(8)]
    psM = ps[0]
    for cchunk in range(12):
        w = wm[cchunk % 2]
        dma("sp", w.t[:, :, :], wmod_d[:, cchunk * 512:(cchunk + 1) * 512].rearrange("(kc p) n -> p kc n", p=128),
            writes=[w.r])
        for (kc, j) in wi_jobs[cchunk * 2:cchunk * 2 + 2]:
            stg = stgR.next()
            dma("sp", stg.t[:, :], win_d[kc * 128:(kc + 1) * 128, j * 2048:(j + 1) * 2048], writes=[stg.r])
            cast("pool", wi[kc][:, j * 2048:(j + 1) * 2048], stg.t[:, :], reads=[stg.r], writes=[wir[kc][j]])
        for ol in range(4):
            oc = cchunk * 4 + ol
            for kc in range(8):
                mm(psM.t[:, oc * 2:oc * 2 + 2], w.t[:, kc, ol * 128:(ol + 1) * 128], sc2.t[:, kc, :],
                   kc == 0, kc == 7, reads=[w.r, sc2.r], writes=[psM.r])
    psM3 = psM.t[:, 0:96].rearrange("p (o j) -> p o j", j=2)
    tt("dve", modL.t[:, :], psM3[:, :, 0], pp.t[:, 0:48], ALU.add, reads=[psM.r, pp.r], writes=[modL.r])
    tt("dve", modC.t[:, :], psM3[:, :, 1], pp.t[:, 0:48], ALU.add, reads=[psM.r, pp.r], writes=[modC.r])
    stt(cols.t[:, 0:8], modL.t[:, 8:16], 1.0, pp.t[:, 48:56], ALU.add, ALU.mult, reads=[modL.r, pp.r], writes=[cols.r])
    stt(cols.t[:, 8:16], modC.t[:, 8:16], 1.0, pp.t[:, 48:56], ALU.add, ALU.mult, reads=[modC.r, pp.r], writes=[cols.r])
    stt(cols.t[:, 16:24], modL.t[:, 32:40], 1.0, pp.t[:, 56:64], ALU.add, ALU.mult, reads=[modL.r, pp.r], writes=[cols.r])
    spl = TB(ar.alloc([128, 20], F32, "spl"))
    act(spl.t[:, :], pp.t[:, 154:174], AF.Exp, reads=[pp.r], writes=[spl.r], scale=-1.0)
    act(spl.t[:, :], spl.t[:, :], AF.Ln, reads=[spl.r], writes=[spl.r], bias=1.0)
    ts1("dve", lc.t[:, 0:20], spl.t[:, :], -4.0, ALU.mult, reads=[spl.r], writes=[lc.r])
    ts1("dve", lc.t[:, 20:40], spl.t[:, :], -8.0, ALU.mult, reads=[spl.r], writes=[lc.r])
    ts1("dve", lc.t[:, 40:60], pp.t[:, 114:134], 0.5, ALU.mult, reads=[pp.r], writes=[lc.r])
    ts1("dve", lc.t[:, 60:80], pp.t[:, 134:154], 0.5, ALU.mult, reads=[pp.r], writes=[lc.r])
    for (kc, j) in wi_jobs[24:]:
        pass

    def make_row(dst, colap, colreg, scale, psb, diag):
        for fc in range(8):
            ts1("dve", diag.t[:, :], ident, colap[:, fc:fc + 1], ALU.mult, reads=[cst.r, colreg], writes=[diag.r])
            mm(psb.t[:, (fc % 4) * 128:(fc % 4 + 1) * 128], onesf.t[:, :], diag.t[:, :], True, True,
               reads=[onesf.r, diag.r], writes=[psb.r])
            if fc % 4 == 3:
                h = fc // 4
                act(dst.t[:, h * 512:(h + 1) * 512], psb.t[:, :], AF.Copy, reads=[psb.r], writes=[dst.r], scale=scale)

    if debug:
        MD_d = nc.dram_tensor("MD", [128, 96], F32, kind="ExternalOutput").ap()
        dma("sp", MD_d[:, 0:48], modL.t[:, :], reads=[modL.r])
        dma("sp", MD_d[:, 48:96], modC.t[:, :], reads=[modC.r])
    S.barrier()
    ar.reset(wi_mark)

    if "A" in phases:
        hTS = [[TB(ar.alloc([128, 512], BF16, "hT")) for _ in range(8)] for _ in range(2)]
        xt = [TB(ar.alloc([128, 1024], F32, "xt")) for _ in range(4)]
        sqj = TB(ar.alloc([128, 1024], BF16, "sqj"))
        cosS = [TB(ar.alloc([128, 512], F32, "cosb")) for _ in range(2)]
        sinS = [TB(ar.alloc([128, 512], F32, "sinb")) for _ in range(2)]
        halo = TB(ar.alloc([128, 10, 4], F32, "halo"))
        ubR = Rot([TB(ar.alloc([128, 516], F32, "ub")) for _ in range(4)])
        xcoR = Rot([TB(ar.alloc([128, 516], F32, "xco")) for _ in range(4)])
        xcbR = Rot([TB(ar.alloc([128, 516], BF16, "xcbo")) for _ in range(4)])
        gsbR = Rot([TB(ar.alloc([128, 512], F32, "gsb")) for _ in range(3)])
        g2R = Rot([TB(ar.alloc([128, 512], F32, "g2")) for _ in range(3)])
        hgoR = Rot([TB(ar.alloc([128, 512], BF16, "hgo")) for _ in range(3)])
        sqR = Rot([TB(ar.alloc([128, 512], BF16, "sq")) for _ in range(2)])
        qgR = Rot([TB(ar.alloc([128, 512], F32, "qg")) for _ in range(3)])
        sdR = Rot([TB(ar.alloc([128, 512], F32, "sd")) for _ in range(2)])
        t1R = Rot([TB(ar.alloc([128, 512], F32, "t1")) for _ in range(2)])
        t2R = Rot([TB(ar.alloc([128, 512], F32, "t2")) for _ in range(2)])
        qoR = Rot([TB(ar.alloc([128, 512], BF16, "qo")) for _ in range(3)])
        toR = Rot([TB(ar.alloc([128, 512], BF16, "to")) for _ in range(4)])
        voR = Rot([TB(ar.alloc([128, 256], BF16, "vo")) for _ in range(2)])
        psT = [ps[0], ps[1]]
        psO = Rot([ps[2], ps[3], ps[4]])
        psS, psR, psV = ps[5], ps[6], ps[7]

        def tile_geom(tti):
            is_ctx = tti == 0
            nc_ = 256 if is_ctx else 512
            col0 = 0 if is_ctx else NCTX + (tti - 1) * 512
            l0 = 0 if is_ctx else (tti - 1) * 512
            return is_ctx, nc_, col0, l0

        halo_r = [Reg() for _ in range(10)]

        def prologue_load(tti):
            is_ctx, nc_, col0, l0 = tile_geom(tti)
            nsub = nc_ // 128
            for s in range(nsub):
                src = ctx_d[s * 128:(s + 1) * 128, :] if is_ctx else x_d[l0 + s * 128:l0 + (s + 1) * 128, :]
                dma("sp", xt[s].t[:, :], src, writes=[xt[s].r])
            if not is_ctx:
                dma("sp", cosS[tti % 2].t[:, :], cos_d[:, l0:l0 + 512], writes=[cosS[tti % 2].r])
                dma("sp", sinS[tti % 2].t[:, :], sin_d[:, l0:l0 + 512], writes=[sinS[tti % 2].r])

        def prologue_stats(tti):
            is_ctx, nc_, col0, l0 = tile_geom(tti)
            nsub = nc_ // 128
            for s in range(nsub):
                act(sqj.t[:, :], xt[s].t[:, :], AF.Square, reads=[xt[s].r], writes=[sqj.r, ss.r], accum_out=ss.t[:, s:s + 1])
            act(rstd.t[:, 0:nsub], ss.t[:, 0:nsub], AF.Sqrt, reads=[ss.r], writes=[rstd.r], scale=1.0 / D, bias=EPS)
            recip(rstd.t[:, 0:nsub], rstd.t[:, 0:nsub], reads=[rstd.r], writes=[rstd.r])
            for s in range(nsub):
                ts1("dve", xt[s].t[:, :], xt[s].t[:, :], rstd.t[:, s:s + 1], ALU.mult, reads=[xt[s].r, rstd.r], writes=[xt[s].r])

        def prologue_compute(tti):
            is_ctx, nc_, col0, l0 = tile_geom(tti)
            nsub = nc_ // 128
            hT = hTS[tti % 2]
            acol = 8 if is_ctx else 0
            bmod = modC if is_ctx else modL
            for fc in range(8):
                bank = psT[fc % 2]
                for s in range(nsub):
                    tr(bank.t[:, s * 128:(s + 1) * 128], xt[s].t[:, fc * 128:(fc + 1) * 128], ident,
                       reads=[xt[s].r, cst.r], writes=[bank.r])
                act(hT[fc].t[:, 0:nc_], bank.t[:, 0:nc_], AF.Identity, reads=[bank.r, cols.r, bmod.r], writes=[hT[fc].r],
                    scale=cols.t[:, acol + fc:acol + fc + 1], bias=bmod.t[:, fc:fc + 1])

        def inproj(hT, oc, nc_):
            p = psO.next()
            third = (oc * 128) // 2048
            for kc in range(8):
                mm(p.t[:, 0:nc_], wi[kc][:, oc * 128:(oc + 1) * 128], hT[kc].t[:, 0:nc_], kc == 0, kc == 7,
                   reads=[wir[kc][third], hT[kc].r], writes=[p.r])
            return p

        def pipelined(n, stage1, stage2):
            state = {}
            if n > 0:
                state[0] = stage1(0)
            for i in range(n):
                if i + 1 < n:
                    state[i + 1] = stage1(i + 1)
                stage2(i, state.pop(i))

        def qk_stage1(hT, oc, nc_, gain_col):
            p = inproj(hT, oc, nc_)
            sq = sqR.next()
            qg = qgR.next()
            act(sq.t[:, 0:nc_], p.t[:, 0:nc_], AF.Square, reads=[p.r], writes=[sq.r])
            act(qg.t[:, 0:nc_], p.t[:, 0:nc_], AF.Identity, reads=[p.r, pp.r], writes=[qg.r], scale=gain_col)
            return sq, qg

        def qk_stage2(st, nc_, rope, cosb, sinb, dst_ap, dst_reg):
            sq, qg = st
            sd = sdR.next()
            qo = qoR.next()
            mm(psS.t[:, 0:nc_], onesb.t[:, :], sq.t[:, 0:nc_], True, True, reads=[onesb.r, sq.r], writes=[psS.r])
            if rope:
                mm(psR.t[:, 0:nc_], perm, qg.t[:, 0:nc_], True, True, reads=[cst.r, qg.r], writes=[psR.r])
            act(sd.t[:, 0:nc_], psS.t[:, 0:nc_], AF.Ln, reads=[psS.r], writes=[sd.r], scale=1.0 / 128.0, bias=EPS)
            act(sd.t[:, 0:nc_], sd.t[:, 0:nc_], AF.Exp, reads=[sd.r], writes=[sd.r], scale=-0.5)
            if rope:
                t1 = t1R.next()
                t2 = t2R.next()
                tt("pool", t1.t[:, 0:nc_], qg.t[:, 0:nc_], cosb.t[:, 0:nc_], ALU.mult, reads=[qg.r, cosb.r], writes=[t1.r])
                tt("dve", t2.t[:, 0:nc_], psR.t[:, 0:nc_], sinb.t[:, 0:nc_], ALU.mult, reads=[psR.r, sinb.r], writes=[t2.r])
                tt("pool", t1.t[:, 0:nc_], t1.t[:, 0:nc_], t2.t[:, 0:nc_], ALU.add, reads=[t1.r, t2.r], writes=[t1.r])
                tt("dve", qo.t[:, 0:nc_], t1.t[:, 0:nc_], sd.t[:, 0:nc_], ALU.mult, reads=[t1.r, sd.r], writes=[qo.r])
            else:
                tt("dve", qo.t[:, 0:nc_], qg.t[:, 0:nc_], sd.t[:, 0:nc_], ALU.mult, reads=[qg.r, sd.r], writes=[qo.r])
            dma("sp", dst_ap, qo.t[:, 0:nc_], reads=[qo.r], writes=[dst_reg])

        def amemz(ap_, writes):
            return S.add("act", lambda e: e.memzero(ap_), (), writes)

        def u_stage1(hT, tti, m):
            is_ctx, nc_, col0, l0 = tile_geom(tti)
            p = inproj(hT, m, nc_)
            ub = ubR.next()
            if is_ctx:
                amemz(ub.t[:, 0:2], [ub.r])
                amemz(ub.t[:, 258:259], [ub.r])
                act(ub.t[:, 2:258], p.t[:, 0:256], AF.Copy, reads=[p.r], writes=[ub.r])
            else:
                if tti == 1:
                    amemz(ub.t[:, 0:3], [ub.r])
                else:
                    cast("act", ub.t[:, 0:3], halo.t[:, m, 0:3], reads=[halo_r[m]], writes=[ub.r])
                if tti == 8:
                    amemz(ub.t[:, 515:516], [ub.r])
                act(ub.t[:, 3:515], p.t[:, 0:512], AF.Copy, reads=[p.r], writes=[ub.r])
                if tti != 8:
                    cast("act", halo.t[:, m, 0:3], ub.t[:, 512:515], reads=[ub.r], writes=[halo_r[m]])
            return ub

        def u_stage2(tti, m, ub):
            is_ctx, nc_, col0, l0 = tile_geom(tti)
            xo = xcoR.next()
            xb = xcbR.next()
            if is_ctx:
                k0, k1, dst0 = 0, 256, 0
            else:
                k0 = 1 if tti == 1 else 0
                k1 = 513 if tti == 8 else 512
                dst0 = NCTX + l0 - 1 + k0
            n = k1 - k0
            for tap in (3, 0, 1, 2):
                usl = ub.t[:, k0 + tap:k0 + tap + n]
                wcol = pp.t[:, 64 + tap * 10 + m:64 + tap * 10 + m + 1]
                if tap == 3:
                    act(xo.t[:, 0:n], usl, AF.Identity, reads=[ub.r, pp.r], writes=[xo.r], scale=wcol, bias=pp.t[:, 104 + m:105 + m])
                else:
                    stt(xo.t[:, 0:n], usl, wcol, xo.t[:, 0:n], ALU.mult, ALU.add, reads=[ub.r, pp.r, xo.r], writes=[xo.r])
            cast("pool", xb.t[:, 0:n], xo.t[:, 0:n], reads=[xo.r], writes=[xb.r])
            dma("sp", XC_d[m * 128:(m + 1) * 128, dst0:dst0 + n], xo.t[:, 0:n], reads=[xo.r], writes=[rXC])
            dma("sp", XCB_d[m * 128:(m + 1) * 128, dst0:dst0 + n], xb.t[:, 0:n], reads=[xb.r], writes=[rXCB])

        K0 = 0.7978845608028654
        K1 = 0.044715

        def g_stage1(hT, m, nc_):
            p = inproj(hT, 10 + m, nc_)
            gsb = gsbR.next()
            g2 = g2R.next()
            act(gsb.t[:, :], p.t[:, :], AF.Copy, reads=[p.r], writes=[gsb.r])
            act(g2.t[:, :], p.t[:, :], AF.Square, reads=[p.r], writes=[g2.r])
            ts("dve", g2.t[:, :], g2.t[:, :], K0 * K1, K0, ALU.mult, ALU.add, reads=[g2.r], writes=[g2.r])
            tt("pool", g2.t[:, :], g2.t[:, :], gsb.t[:, :], ALU.mult, reads=[g2.r, gsb.r], writes=[g2.r])
            return gsb, g2

        def g_stage2(m, l0, st):
            gsb, g2 = st
            hgo = hgoR.next()
            act(g2.t[:, :], g2.t[:, :], AF.Tanh, reads=[g2.r], writes=[g2.r])
            stt(hgo.t[:, :], g2.t[:, :], 1.0, gsb.t[:, :], ALU.add, ALU.mult, reads=[g2.r, gsb.r], writes=[hgo.r])
            dma("sp", GG_d[m * 128:(m + 1) * 128, l0:l0 + 512], hgo.t[:, :], reads=[hgo.r], writes=[rGG])

        prologue_load(0)
        prologue_stats(0)
        prologue_compute(0)
        for tti in range(ntiles):
            is_ctx, nc_, col0, l0 = tile_geom(tti)
            nsub = nc_ // 128
            hT = hTS[tti % 2]
            cosb, sinb = cosS[tti % 2], sinS[tti % 2]
            if tti + 1 < ntiles:
                prologue_load(tti + 1)
            def u2(m, ub, tti=tti):
                u_stage2(tti, m, ub)
                if m == 4 and tti + 1 < ntiles:
                    prologue_stats(tti + 1)
            pipelined(10 if "u" in secs else 0, lambda m, hT=hT, tti=tti: u_stage1(hT, tti, m), u2)
            if tti + 1 < ntiles:
                prologue_compute(tti + 1)
            pipelined(2 if "k" in secs else 0,
                      lambda hk, hT=hT, nc_=nc_: qk_stage1(hT, 28 + hk, nc_, pp.t[:, 175:176]),
                      lambda hk, st, nc_=nc_, is_ctx=is_ctx, cosb=cosb, sinb=sinb, col0=col0:
                      qk_stage2(st, nc_, not is_ctx, cosb, sinb, KT_d[hk * 128:(hk + 1) * 128, col0:col0 + nc_], rKT))
            for s in range(nsub if "v" in secs else 0):
                for kc in range(8):
                    mm(psV.t[:, 0:256], hT[kc].t[:, s * 128:(s + 1) * 128], wi[kc][:, 3840:4096], kc == 0, kc == 7,
                       reads=[hT[kc].r, wir[kc][1]], writes=[psV.r])
                vo = voR.next()
                act(vo.t[:, :], psV.t[:, 0:256], AF.Copy, reads=[psV.r], writes=[vo.r])
                dma("sp", V_d[col0 + s * 128:col0 + (s + 1) * 128, :], vo.t[:, :], reads=[vo.r], writes=[rV])
            if is_ctx:
                continue
            pipelined(8 if "q" in secs else 0,
                      lambda h, hT=hT, nc_=nc_: qk_stage1(hT, 20 + h, nc_, pp.t[:, 174:175]),
                      lambda h, st, nc_=nc_, cosb=cosb, sinb=sinb, l0=l0:
                      qk_stage2(st, nc_, True, cosb, sinb, Q_d[h * 128:(h + 1) * 128, l0:l0 + 512], rQ))
            pipelined(10 if "g" in secs else 0,
                      lambda m, hT=hT, nc_=nc_: g_stage1(hT, m, nc_),
                      lambda m, st, l0=l0: g_stage2(m, l0, st))
            for gi in range(16 if "t" in secs else 0):
                p = inproj(hT, 32 + gi, nc_)
                to = toR.next()
                act(to.t[:, :], p.t[:, :], AF.Tanh, reads=[p.r], writes=[to.r], scale=0.5)
                if gi < 8:
                    dma("sp", TA_d[gi * 128:(gi + 1) * 128, l0:l0 + 512], to.t[:, :], reads=[to.r], writes=[rTA])
                else:
                    dma("sp", TBd_d[(gi - 8) * 128:(gi - 7) * 128, l0:l0 + 512], to.t[:, :], reads=[to.r], writes=[rTBd])
        S.barrier()
    ar.reset(persist_mark)

    if "B" in phases:
        xc32 = [TB(ar.alloc([128, T], F32, "xc32")) for _ in range(2)]
        xcb = [TB(ar.alloc([128, T], BF16, "xcb")) for _ in range(4)]
        abuf = [TB(ar.alloc([128, T], F32, "abuf")) for _ in range(2)]
        a2buf = [TB(ar.alloc([128, T], F32, "a2buf")) for _ in range(2)]
        tib = [TB(ar.alloc([128, T], F32, "tib")) for _ in range(2)]
        gwst = TB(ar.alloc([128, 12, 128], F32, "gwst"))
        gws = [TB(ar.alloc([128, 12, 128], BF16, "gw")) for _ in range(2)]
        ggb = TB(ar.alloc([128, L], BF16, "gg"))
        trR = Rot([TB(ar.alloc([128, 1024], F32, "trt")) for _ in range(2)])
        pairR = Rot([0, 1, 2, 3])
        wtiles = [(0, 256)] + [(NCTX + i * 1024, 1024) for i in range(4)]
        abR = [[Reg() for _ in range(5)] for _ in range(2)]
        a2R = [[Reg() for _ in range(5)] for _ in range(2)]
        tiR = [[Reg() for _ in range(5)] for _ in range(2)]
        loaded = {"xcb": -1, "xc32": -1}

        def load_xcb(upto):
            while loaded["xcb"] < min(upto, 9):
                loaded["xcb"] += 1
                j = loaded["xcb"]
                dma("sp", xcb[j % 4].t[:, :], XCB_d[j * 128:(j + 1) * 128, :], reads=[rXCB], writes=[xcb[j % 4].r])

        def load_xc32(upto):
            while loaded["xc32"] < min(upto, 9):
                loaded["xc32"] += 1
                j = loaded["xc32"]
                dma("sp", xc32[j % 2].t[:, :], XC_d[j * 128:(j + 1) * 128, :], reads=[rXC], writes=[xc32[j % 2].r])

        def load_gw(m):
            ks = gate_kchunks(m)
            for mat in range(4):
                for ki, k in enumerate(ks):
                    dma("sp", gwst.t[:, mat * 3 + ki, :], wg_d[mat, k * 128:(k + 1) * 128, m * 128:(m + 1) * 128], writes=[gwst.r])

        load_gw(0)
        for m in range(10):
            ks = gate_kchunks(m)
            gw = gws[m % 2]
            if m == 0:
                load_xcb(1)
                load_xc32(0)
            load_xcb(m + 2)
            load_xc32(m + 1)
            for mat in range(4):
                cast("pool", gw.t[:, mat * 3:mat * 3 + len(ks), :], gwst.t[:, mat * 3:mat * 3 + len(ks), :], reads=[gwst.r], writes=[gw.r])
            if m + 1 < 10:
                load_gw(m + 1)
            if m > 0:
                dma("sp", Z_d[(m - 1) * 128:m * 128, :], ggb.t[:, :], reads=[ggb.r], writes=[rZ])
            dma("sp", ggb.t[:, :], GG_d[m * 128:(m + 1) * 128, :], reads=[rGG], writes=[ggb.r])
            x32 = xc32[m % 2]
            tbufs = {}
            for d in range(2):
                ab = abuf[d]
                a2 = a2buf[d]
                ti = tib[d if m % 2 == 0 else 1 - d]
                tbufs[d] = ti
                abr = abR[d]
                a2r = a2R[d]
                tir = tiR[d if m % 2 == 0 else 1 - d]
                ci = d * 10 + m
                for wi_, (c0, n) in enumerate(wtiles):
                    ia = pairR.next()
                    ix_ = pairR.next()
                    for gate, pi in ((0, ia), (1, ix_)):
                        for half in range((n + 511) // 512):
                            hn = min(512, n - half * 512)
                            bank = ps[2 * pi + half]
                            for ki, k in enumerate(ks):
                                mm(bank.t[:, 0:hn], gw.t[:, (d * 2 + gate) * 3 + ki, :],
                                   xcb[k % 4].t[:, c0 + half * 512:c0 + half * 512 + hn], ki == 0, ki == len(ks) - 1,
                                   reads=[gw.r, xcb[k % 4].r], writes=[bank.r])
                    nb = (n + 511) // 512
                    ra = [ps[2 * ia + h].r for h in range(nb)]
                    rx = [ps[2 * ix_ + h].r for h in range(nb)]
                    trt = trR.next()
                    act(trt.t[:, 0:n], ps2[ia][:, 0:n], AF.Tanh, reads=ra + [lc.r], writes=[trt.r],
                        scale=0.5, bias=lc.t[:, 40 + ci:41 + ci])
                    act(ab.t[:, c0:c0 + n], trt.t[:, 0:n], AF.Exp, reads=[trt.r, lc.r], writes=[abr[wi_]],
                        scale=lc.t[:, ci:ci + 1], bias=lc.t[:, ci:ci + 1])
                    act(a2.t[:, c0:c0 + n], ab.t[:, c0:c0 + n], AF.Square, reads=[abr[wi_]], writes=[a2r[wi_]])
                    act(ti.t[:, c0:c0 + n], ps2[ix_][:, 0:n], AF.Tanh, reads=rx + [lc.r], writes=[tir[wi_]],
                        scale=0.5, bias=lc.t[:, 60 + ci:61 + ci])
                act(a2.t[:, :], a2.t[:, :], AF.Sqrt, reads=a2r, writes=a2r, scale=-0.25, bias=0.25)
                stt(ti.t[:, :], ti.t[:, :], 1.0, x32.t[:, :], ALU.add, ALU.mult, reads=tir + [x32.r], writes=tir)
                tt("dve", a2.t[:, :], a2.t[:, :], ti.t[:, :], ALU.mult, reads=a2r + tir, writes=a2r)
                if d == 0:
                    scan(ti.t[:, :], ab.t[:, :], a2.t[:, :], 0.0, reads=abr + a2r, writes=tir)
                else:
                    scan(rev_ap(x32.t[:, 0:NCTX]), rev_ap(ab.t[:, 0:NCTX]), rev_ap(a2.t[:, 0:NCTX]), 0.0,
                         reads=abr + a2r + tir, writes=[x32.r])
                    scan(rev_ap(x32.t[:, NCTX:T]), rev_ap(ab.t[:, NCTX:T]), rev_ap(a2.t[:, NCTX:T]), x32.t[:, 0:1],
                         reads=abr + a2r + [x32.r], writes=[x32.r])
            hf = tbufs[0]
            hfr = tiR[0 if m % 2 == 0 else 1]
            tt("dve", hf.t[:, NCTX:T], hf.t[:, NCTX:T], x32.t[:, NCTX:T], ALU.add,
               reads=hfr + [x32.r], writes=hfr)
            stt(ggb.t[:, :], hf.t[:, NCTX:T], 0.5, ggb.t[:, :], ALU.mult, ALU.mult, reads=hfr + [ggb.r], writes=[ggb.r])
        dma("sp", Z_d[9 * 128:10 * 128, :], ggb.t[:, :], reads=[ggb.r], writes=[rZ])
        S.barrier()
        ar.reset(persist_mark)

    if "C" in phases:
        KT = [TB(ar.alloc([128, T], BF16, "KT")) for _ in range(2)]
        Vt = TB(ar.alloc([128, 34, 256], BF16, "Vt"))
        woa = [TB(ar.alloc([128, D], BF16, "woa")) for _ in range(8)]
        wol = [TB(ar.alloc([128, D], BF16, "wol")) for _ in range(10)]
        wo = [TB(ar.alloc([128, D], BF16, "wo")) for _ in range(8)]
        stgC = Rot([TB(ar.alloc([128, D], F32, "stgc")) for _ in range(2)])
        qt = TB(ar.alloc([128, 8, 512], BF16, "qt"))
        zt = TB(ar.alloc([128, 10, 512], BF16, "zt"))
        tat = TB(ar.alloc([128, 8, 512], BF16, "tat"))
        tbt = TB(ar.alloc([128, 8, 512], BF16, "tbt"))
        xtS = [[TB(ar.alloc([128, D], F32, "xtc")) for _ in range(4)] for _ in range(2)]

        def loads_c(j):
            c0 = j * 512
            dma("sp", qt.t[:, :, :], Q_d[:, c0:c0 + 512].rearrange("(h p) n -> p h n", p=128), reads=[rQ], writes=[qt.r])
            if j == 0:
                dma("sp", KT[0].t[:, :], KT_d[0:128, :], reads=[rKT], writes=[KT[0].r])
                dma("sp", Vt.t[:, :, :], V_d.rearrange("(c p) n -> p c n", p=128), reads=[rV], writes=[Vt.r])
                dma("sp", KT[1].t[:, :], KT_d[128:256, :], reads=[rKT], writes=[KT[1].r])
            dma("sp", tbt.t[:, :, :], TBd_d[:, c0:c0 + 512].rearrange("(h p) n -> p h n", p=128), reads=[rTBd], writes=[tbt.r])
            dma("sp", zt.t[:, :, :], Z_d[:, c0:c0 + 512].rearrange("(h p) n -> p h n", p=128), reads=[rZ], writes=[zt.r])
            dma("sp", tat.t[:, :, :], TA_d[:, c0:c0 + 512].rearrange("(h p) n -> p h n", p=128), reads=[rTA], writes=[tat.r])
            for s in range(4):
                dma("sp", xtS[j % 2][s].t[:, :], x_d[c0 + s * 128:c0 + (s + 1) * 128, :], writes=[xtS[j % 2][s].r])

        loads_c(0)
        for (wl, src, nk) in ((woa, woa_d, 8), (wol, wol_d, 10), (wo, wo_d, 8)):
            for kc in range(nk):
                stg = stgC.next()
                dma("sp", stg.t[:, :], src[kc * 128:(kc + 1) * 128, :], writes=[stg.r])
                cast("pool", wl[kc].t[:, :], stg.t[:, :], reads=[stg.r], writes=[wl[kc].r])
        gmh = TB(ar.alloc([128, D], F32, "gmh"))
        diag = TB(ar.alloc([128, 128], F32, "diag"))
        make_row(gmh, modL.t[:, 16:24], modL.r, 0.5, ps[7], diag)
        att = [TB(ar.alloc([128, 512], BF16, "att")) for _ in range(8)]
        mBR = Rot([TB(ar.alloc([128, 512], F32, "mB")) for _ in range(2)])
        mt = [TB(ar.alloc([128, 512], BF16, "mt")) for _ in range(8)]
        ptR = Rot([TB(ar.alloc([128, 1024], BF16, "pt")) for _ in range(3)])
        rdR = Rot([TB(ar.alloc([128, 512], F32, "rden")) for _ in range(2)])
        tmR = Rot([TB(ar.alloc([128, 512], F32, "tmpc")) for _ in range(2)])
        spR = Rot([0, 1])
        psOa = Rot([ps[4], ps[5]])
        psDa = Rot([ps[6], ps[7]])
        psP = Rot([ps[0], ps[1], ps[2], ps[3]])
        SCL = 1.0 / float(np.sqrt(128.0))
        items = [(h, pr) for h in range(8) for pr in range(17)]
        cvo = Rot([TB(ar.alloc([128, D], BF16, "cvo")) for _ in range(2)])
        cjobs = []
        for kc in range(8):
            for q0 in range(0, 2 * HID, 1024):
                q1 = min(q0 + 1024, 2 * HID)
                cjobs.append((wfi_d[kc * 128:(kc + 1) * 128, q0:q1], WFI_d[kc * 128:(kc + 1) * 128, q0:q1], q1 - q0, rWFI))
        for kc in range(22):
            cjobs.append((wfo_d[kc * 128:(kc + 1) * 128, :], WFO_d[kc * 128:(kc + 1) * 128, :], D, rWFO))
        per_tile = (len(cjobs) + 7) // 8
        for j in range(8):
            c0 = j * 512
            xt = xtS[j % 2]
            for (src, dst, n, rg) in cjobs[j * per_tile:(j + 1) * per_tile]:
                stg = stgC.next()
                co = cvo.next()
                dma("sp", stg.t[:, 0:n], src, writes=[stg.r])
                cast("pool", co.t[:, 0:n], stg.t[:, 0:n], reads=[stg.r], writes=[co.r])
                dma("sp", dst, co.t[:, 0:n], reads=[co.r], writes=[rg])
            st = {}
            cur = {}

            def emit_S(i):
                h, pr = items[i]
                hk = h // 4
                pb = spR.next()
                for half in range(2):
                    kc = pr * 2 + half
                    bank = ps[2 * pb + half]
                    mm(bank.t[:, :], KT[hk].t[:, kc * 128:(kc + 1) * 128], qt.t[:, h, :], True, True,
                       reads=[KT[hk].r, qt.r], writes=[bank.r])
                st[i] = pb

            def emit_rest(i):
                h, pr = items[i]
                hk = h // 4
                pb = st[i]
                if pr == 0:
                    cur["po"] = psOa.next()
                    cur["pd"] = psDa.next()
                po, pd = cur["po"], cur["pd"]
                pt = ptR.next()
                act(pt.t[:, :], ps2[pb][:, :], AF.Exp, reads=[ps[2 * pb].r, ps[2 * pb + 1].r], writes=[pt.r], scale=SCL)
                for half in range(2):
                    kc = pr * 2 + half
                    mm(po.t[:, :], Vt.t[:, kc, hk * 128:(hk + 1) * 128], pt.t[:, half * 512:(half + 1) * 512], kc == 0, kc == 33,
                       reads=[Vt.r, pt.r], writes=[po.r])
                    mm(pd.t[:, :], onesb.t[:, :], pt.t[:, half * 512:(half + 1) * 512], kc == 0, kc == 33,
                       reads=[onesb.r, pt.r], writes=[pd.r])
                if pr == 16:
                    rd = rdR.next()
                    recip(rd.t[:, :], pd.t[:, :], reads=[pd.r], writes=[rd.r])
                    tt("dve", att[h].t[:, :], po.t[:, :], rd.t[:, :], ALU.mult, reads=[po.r, rd.r], writes=[att[h].r])

            emit_S(0)
            emit_S(1)
            for i in range(len(items)):
                emit_rest(i)
                if i + 2 < len(items):
                    emit_S(i + 2)
            for oc in range(8):
                p = psP.next()
                for kc in range(8):
                    mm(p.t[:, :], woa[kc].t[:, oc * 128:(oc + 1) * 128], att[kc].t[:, :], kc == 0, kc == 7,
                       reads=[woa[kc].r, att[kc].r], writes=[p.r])
                mB = mBR.next()
                stt(mB.t[:, :], tbt.t[:, oc, :], 1.0, p.t[:, :], ALU.add, ALU.mult, reads=[tbt.r, p.r], writes=[mB.r])
                p = psP.next()
                for kc in range(10):
                    mm(p.t[:, :], wol[kc].t[:, oc * 128:(oc + 1) * 128], zt.t[:, kc, :], kc == 0, kc == 9,
                       reads=[wol[kc].r, zt.r], writes=[p.r])
                tm = tmR.next()
                stt(tm.t[:, :], tat.t[:, oc, :], 1.0, p.t[:, :], ALU.add, ALU.mult, reads=[tat.r, p.r], writes=[tm.r])
                tt("pool", mt[oc].t[:, :], tm.t[:, :], mB.t[:, :], ALU.add, reads=[tm.r, mB.r], writes=[mt[oc].r])
            if j + 1 < 8:
                loads_c(j + 1)
            for s in range(4):
                for fh in range(2):
                    p = psP.next()
                    for kc in range(8):
                        mm(p.t[:, :], mt[kc].t[:, s * 128:(s + 1) * 128], wo[kc].t[:, fh * 512:(fh + 1) * 512], kc == 0, kc == 7,
                           reads=[mt[kc].r, wo[kc].r], writes=[p.r])
                    tm = tmR.next()
                    tt("dve", tm.t[:, :], p.t[:, :], gmh.t[:, fh * 512:(fh + 1) * 512], ALU.mult, reads=[p.r, gmh.r], writes=[tm.r])
                    tt("pool", xt[s].t[:, fh * 512:(fh + 1) * 512], tm.t[:, :], xt[s].t[:, fh * 512:(fh + 1) * 512], ALU.add,
                       reads=[tm.r, xt[s].r], writes=[xt[s].r])
                dma("sp", X1_d[c0 + s * 128:c0 + (s + 1) * 128, :], xt[s].t[:, :], reads=[xt[s].r], writes=[rX1])
        S.barrier()
        ar.reset(persist_mark)

    if "D" in phases:
        wfi = [ar.alloc([128, 2 * HID], BF16, "wfi") for _ in range(8)]
        wfir = [[Reg() for _ in range(8)] for _ in range(8)]
        wfo = [ar.alloc([128, D], BF16, "wfo") for _ in range(22)]
        wfor = [Reg() for _ in range(22)]
        xs = [TB(ar.alloc([128, D], F32, "xs")) for _ in range(4)]

        def loads_d(j):
            c0 = j * 512
            for s in range(4):
                dma("sp", xs[s].t[:, :], X1_d[c0 + s * 128:c0 + (s + 1) * 128, :], reads=[rX1], writes=[xs[s].r])

        loads_d(0)
        cuts = [0, 768, 1536, 2176, HID]
        wgroups = []
        for gi_ in range(4):
            wgroups.append((cuts[gi_], cuts[gi_ + 1]))
            wgroups.append((HID + cuts[gi_], HID + cuts[gi_ + 1]))

        def wfi_reg(kc, col):
            for gi, (g0, g1) in enumerate(wgroups):
                if g0 <= col < g1:
                    return wfir[kc][gi]

        for gi, (g0, g1) in enumerate(wgroups):
            for kc in range(8):
                dma("sp", wfi[kc][:, g0:g1], WFI_d[kc * 128:(kc + 1) * 128, g0:g1], reads=[rWFI], writes=[wfir[kc][gi]])
        for kc in range(22):
            dma("sp", wfo[kc][:, :], WFO_d[kc * 128:(kc + 1) * 128, :], reads=[rWFO], writes=[wfor[kc]])
        gfr = TB(ar.alloc([128, D], F32, "gfr"))
        diag = TB(ar.alloc([128, 128], F32, "diagd"))
        make_row(gfr, modL.t[:, 40:48], modL.r, 1.0, ps[7], diag)
        sqj = TB(ar.alloc([128, D], BF16, "sqjd"))
        hfT = [TB(ar.alloc([128, 512], BF16, "hfT")) for _ in range(8)]
        aT = [TB(ar.alloc([128, 512], BF16, "aT")) for _ in range(22)]
        sgR = Rot([TB(ar.alloc([128, 512], F32, "sg")) for _ in range(2)])
        x1l = TB(ar.alloc([128, D], F32, "x1l"))
        tmR = Rot([TB(ar.alloc([128, 512], F32, "tmpd")) for _ in range(2)])
        psT = [ps[0], ps[1]]
        psGa = Rot([ps[2], ps[3]])
        psUp = Rot([ps[4], ps[5]])
        psF = Rot([ps[6], ps[7]])

        for j in range(8):
            c0 = j * 512
            for s in range(4):
                act(sqj.t[:, :], xs[s].t[:, :], AF.Square, reads=[xs[s].r], writes=[sqj.r, ss.r], accum_out=ss.t[:, s:s + 1])
            act(rstd.t[:, 0:4], ss.t[:, 0:4], AF.Sqrt, reads=[ss.r], writes=[rstd.r], scale=1.0 / D, bias=EPS)
            recip(rstd.t[:, 0:4], rstd.t[:, 0:4], reads=[rstd.r], writes=[rstd.r])
            for s in range(4):
                ts1("dve", xs[s].t[:, :], xs[s].t[:, :], rstd.t[:, s:s + 1], ALU.mult, reads=[xs[s].r, rstd.r], writes=[xs[s].r])
            for fc in range(8):
                bank = psT[fc % 2]
                for s in range(4):
                    tr(bank.t[:, s * 128:(s + 1) * 128], xs[s].t[:, fc * 128:(fc + 1) * 128], ident,
                       reads=[xs[s].r, cst.r], writes=[bank.r])
                act(hfT[fc].t[:, :], bank.t[:, :], AF.Identity, reads=[bank.r, cols.r, modL.r], writes=[hfT[fc].r],
                    scale=cols.t[:, 16 + fc:17 + fc], bias=modL.t[:, 24 + fc:25 + fc])
            if j + 1 < 8:
                loads_d(j + 1)
            for c in range(22):
                pg = psGa.next()
                pu = psUp.next()
                for kc in range(8):
                    mm(pg.t[:, :], wfi[kc][:, c * 128:(c + 1) * 128], hfT[kc].t[:, :], kc == 0, kc == 7,
                       reads=[wfi_reg(kc, c * 128), hfT[kc].r], writes=[pg.r])
                for kc in range(8):
                    mm(pu.t[:, :], wfi[kc][:, HID + c * 128:HID + (c + 1) * 128], hfT[kc].t[:, :], kc == 0, kc == 7,
                       reads=[wfi_reg(kc, HID + c * 128), hfT[kc].r], writes=[pu.r])
                sg = sgR.next()
                act(sg.t[:, :], pg.t[:, :], AF.Silu, reads=[pg.r], writes=[sg.r])
                tt("dve", aT[c].t[:, :], sg.t[:, :], pu.t[:, :], ALU.mult, reads=[sg.r, pu.r], writes=[aT[c].r])
            for s in range(4):
                dma("sp", x1l.t[:, :], X1_d[c0 + s * 128:c0 + (s + 1) * 128, :], reads=[rX1], writes=[x1l.r])
                for fh in range(2):
                    p = psF.next()
                    for kc in range(22):
                        mm(p.t[:, :], aT[kc].t[:, s * 128:(s + 1) * 128], wfo[kc][:, fh * 512:(fh + 1) * 512], kc == 0, kc == 21,
                           reads=[aT[kc].r, wfor[kc]], writes=[p.r])
                    tm = tmR.next()
                    tt("dve", tm.t[:, :], p.t[:, :], gfr.t[:, fh * 512:(fh + 1) * 512], ALU.mult, reads=[p.r, gfr.r], writes=[tm.r])
                    tt("pool", x1l.t[:, fh * 512:(fh + 1) * 512], tm.t[:, :], x1l.t[:, fh * 512:(fh + 1) * 512], ALU.add,
                       reads=[tm.r, x1l.r], writes=[x1l.r])
                dma("sp", out_d[c0 + s * 128:c0 + (s + 1) * 128, :], x1l.t[:, :], reads=[x1l.r])

    S.finish()
    S.emit(nc)
    nc._sched_stats = dict(nsig=S.nsig, n_ops={e: len(S.streams[e]) for e in ALLE}, sbuf_peak=ar.peak)
    return nc


def _col(v):
    v = np.asarray(v, np.float32)
    return np.ascontiguousarray(v.reshape(-1, 128).T)


def _rope_tables():
    rows = L // 64
    row = np.repeat(np.arange(rows, dtype=np.float32), 64)
    col = np.tile(np.arange(64, dtype=np.float32), rows)
    inv_freq = (np.float32(10000.0) ** (-np.arange(0, 64, 2, dtype=np.float32) / np.float32(64))).astype(np.float32)
    ang_r = (row[:, None] * inv_freq[None, :]).astype(np.float32)
    ang_c = (col[:, None] * inv_freq[None, :]).astype(np.float32)
    cosT = np.zeros((128, L), np.float32)
    sinT = np.zeros((128, L), np.float32)
    for q, (ang, sgn) in enumerate(((ang_r, -1.0), (ang_r, 1.0), (ang_c, -1.0), (ang_c, 1.0))):
        cosT[q * 32:(q + 1) * 32, :] = np.cos(ang).T
        sinT[q * 32:(q + 1) * 32, :] = sgn * np.sin(ang).T
    return cosT, sinT


def _consts():
    cst = np.zeros((128, 256), np.float32)
    cst[:, 0:128] = np.eye(128, dtype=np.float32)
    for m in range(128):
        q = m // 32
        partner = m + 32 if q in (0, 2) else m - 32
        cst[partner, 128 + m] = 1.0
    return cst


def make_in_maps(inputs):
    g = lambda k: np.asarray(inputs[k], np.float32)
    x, c, ctx, c_ctx = g("x"), g("c"), g("ctx"), g("c_ctx")
    w_mod = np.ascontiguousarray(g("w_mod")[0])
    w_in = np.ascontiguousarray(g("w_in")[0])
    lru_wa, lru_wx = g("lru_wa")[0], g("lru_wx")[0]
    wg = np.zeros((4, LW, LW), np.float32)
    for d in range(2):
        for n in range(8):
            sl = slice(160 * n, 160 * (n + 1))
            wg[d * 2 + 0, sl, sl] = lru_wa[d, n]
            wg[d * 2 + 1, sl, sl] = lru_wx[d, n]
    cosT, sinT = _rope_tables()
    cst = _consts()
    base = np.zeros((128, NP), np.float32)
    base[:, 0:48] = _col(g("b_mod")[0])
    base[:, 48:56] = _col(g("norm_mix")[0])
    base[:, 56:64] = _col(g("norm_ffn")[0])
    cw = g("conv_w")[0]
    for j in range(4):
        base[:, 64 + j * 10:64 + (j + 1) * 10] = _col(cw[j])
    base[:, 104:114] = _col(g("conv_b")[0])
    for d in range(2):
        base[:, 114 + d * 10:124 + d * 10] = _col(g("lru_ba")[0][d])
        base[:, 134 + d * 10:144 + d * 10] = _col(g("lru_bx")[0][d])
        base[:, 154 + d * 10:164 + d * 10] = _col(g("lru_lambda")[0][d])
    base[:, 174] = g("q_norm")[0]
    base[:, 175] = g("k_norm")[0]
    base[:, 184:192] = _col(c_ctx)
    shared = dict(
        w_mod=w_mod, w_in=w_in, wg=wg,
        w_out_lru=np.ascontiguousarray(g("w_out_lru")[0]),
        w_out_attn=np.ascontiguousarray(g("w_out_attn")[0]),
        w_out=np.ascontiguousarray(g("w_out")[0]),
        w_ffn_in=np.ascontiguousarray(g("w_ffn_in")[0]),
        w_ffn_out=np.ascontiguousarray(g("w_ffn_out")[0]),
        cst=cst, cosT=cosT, sinT=sinT,
    )
    maps = []
    for b in range(8):
        pp = base.copy()
        pp[:, 176:184] = _col(c[b])
        m = dict(shared)
        m["x"] = np.ascontiguousarray(x[b])
        m["ctx"] = np.ascontiguousarray(ctx[b])
        m["pp"] = pp
        maps.append(m)
    return maps


_NC_CACHE = {}


def kernel(**inputs):
    if "nc" not in _NC_CACHE:
        _NC_CACHE["nc"] = build_nc()
    nc = _NC_CACHE["nc"]
    in_maps = make_in_maps(inputs)
    res = run_bass_kernel_spmd(nc, in_maps, core_ids=list(range(8)))
    out = np.stack([np.asarray(r["out"], np.float32) for r in res.results], axis=0)
    return out
```
